# Optimizing a Trainium2 kernel written in Bass

```python
import jax, jax.numpy as jnp
from jax import lax
import numpy as np

D_MODEL = 2048
BATCH = 4
SEQ = 2048
DEPTH = 2
DEC_BATCH = 32
DEC_SEQ = 16
PAST_LEN = 1024

CHUNK = 64
N_MIXERS = 2
N_CONV_LAYERS = (DEPTH + 1) // 2
N_SGU_LAYERS = DEPTH // 2
CONV_W = 3
CONV_GROUPS = 16
SGU_CHUNK = 128
SGU_WIDTH = D_MODEL
SGU_GROUPS = 16
SGU_GROUP_DIM = SGU_WIDTH // SGU_GROUPS
D_FF = ((8 * D_MODEL // 3 + 255) // 256) * 256
EPS = 1e-6

kernel_name = "hybrid_shortconv_chunk_sgu_stream_step"


def rms_norm(x, g):
    xf = x.astype(jnp.float32)
    y = xf * lax.rsqrt(jnp.mean(xf * xf, axis=-1, keepdims=True) + EPS)
    return (y * g.astype(jnp.float32)).astype(x.dtype)


def layer_norm(x, g, b):
    xf = x.astype(jnp.float32)
    mu = jnp.mean(xf, axis=-1, keepdims=True)
    xc = xf - mu
    y = xc * lax.rsqrt(jnp.mean(xc * xc, axis=-1, keepdims=True) + EPS)
    return (y * g.astype(jnp.float32) + b.astype(jnp.float32)).astype(x.dtype)


def short_conv_mixer(h, hist, w_in, conv_w, w_out):
    L = h.shape[1]
    proj = h @ w_in
    gate_b, gate_c, z = jnp.split(proj, 3, axis=-1)
    cz = gate_c * z
    xp = jnp.concatenate([hist.astype(cz.dtype), cz], axis=1)
    conv = conv_w[0] * xp[:, 0:L] + conv_w[1] * xp[:, 1:L + 1] + conv_w[2] * xp[:, 2:L + 2]
    y = (gate_b * conv) @ w_out
    return y, xp[:, L:]


def chunk_sgu_mixer(h, w_in, b_in, ln_g, ln_b, w_s, b_s, w_out):
    bsz, L, _ = h.shape
    n = min(L, SGU_CHUNK)
    zz = jax.nn.gelu(h @ w_in + b_in, approximate=False)
    u, v = jnp.split(zz, 2, axis=-1)
    v = layer_norm(v, ln_g, ln_b)
    vb = v.reshape(bsz, L // n, n, SGU_GROUPS, SGU_GROUP_DIM)
    w = jnp.tril(w_s[:, :n, :n])
    mixed = jnp.einsum('gij,bcjgd->bcigd', w, vb) + b_s[:, :n].T[None, None, :, :, None]
    y = (u * mixed.reshape(bsz, L, SGU_WIDTH)) @ w_out
    return y, v


def swiglu_ffn(h, w_gate, w_up, w_down):
    return (jax.nn.silu(h @ w_gate) * (h @ w_up)) @ w_down


def setup_inputs(seed: int = 0) -> dict:
    key = jax.random.key(seed)
    ks = jax.random.split(key, 24)
    f32 = jnp.float32
    nrm = lambda k, shape, scale: jax.random.normal(k, shape, f32) * scale
    D = D_MODEL
    return {
        "x_prompt": nrm(ks[0], (BATCH, SEQ, D), 1.0),
        "x_sample": nrm(ks[1], (DEC_BATCH, DEC_SEQ, D), 1.0),
        "cache_conv": nrm(ks[2], (N_CONV_LAYERS, DEC_BATCH, CONV_W - 1, D), 1.0),
        "norm_mix_pre": 1.0 + nrm(ks[3], (DEPTH, D), 0.05),
        "norm_mix_post": 1.0 + nrm(ks[4], (DEPTH, D), 0.05),
        "norm_ffn_pre": 1.0 + nrm(ks[5], (DEPTH, D), 0.05),
        "norm_ffn_post": 1.0 + nrm(ks[6], (DEPTH, D), 0.05),
        "a_w_in": nrm(ks[7], (N_CONV_LAYERS, D, 3 * D), D ** -0.5),
        "a_conv_w": nrm(ks[8], (N_CONV_LAYERS, CONV_W, D), CONV_W ** -0.5),
        "a_w_out": nrm(ks[9], (N_CONV_LAYERS, D, D), D ** -0.5),
        "b_w_in": nrm(ks[10], (N_SGU_LAYERS, D, 2 * SGU_WIDTH), D ** -0.5),
        "b_b_in": nrm(ks[11], (N_SGU_LAYERS, 2 * SGU_WIDTH), 0.02),
        "b_ln_g": 1.0 + nrm(ks[12], (N_SGU_LAYERS, SGU_WIDTH), 0.05),
        "b_ln_b": nrm(ks[13], (N_SGU_LAYERS, SGU_WIDTH), 0.02),
        "b_w_s": nrm(ks[14], (N_SGU_LAYERS, SGU_GROUPS, SGU_CHUNK, SGU_CHUNK), SGU_CHUNK ** -0.5),
        "b_b_s": 1.0 + nrm(ks[15], (N_SGU_LAYERS, SGU_GROUPS, SGU_CHUNK), 0.1),
        "b_w_out": nrm(ks[16], (N_SGU_LAYERS, SGU_WIDTH, D), SGU_WIDTH ** -0.5),
        "ffn_w_gate": nrm(ks[17], (DEPTH, D, D_FF), D ** -0.5),
        "ffn_w_up": nrm(ks[18], (DEPTH, D, D_FF), D ** -0.5),
        "ffn_w_down": nrm(ks[19], (DEPTH, D_FF, D), D_FF ** -0.5),
    }


def reference(x_prompt, x_sample, cache_conv, norm_mix_pre, norm_mix_post, norm_ffn_pre,
              norm_ffn_post, a_w_in, a_conv_w, a_w_out, b_w_in, b_b_in, b_ln_g, b_ln_b,
              b_w_s, b_b_s, b_w_out, ffn_w_gate, ffn_w_up, ffn_w_down):
    xp, xs = x_prompt, x_sample
    conv_prompt, conv_sample, sgu_sample = [], [], []
    for i in range(DEPTH):
        j = i // N_MIXERS
        hp = rms_norm(xp, norm_mix_pre[i])
        hs = rms_norm(xs, norm_mix_pre[i])
        if i % N_MIXERS == 0:
            zero_hist = jnp.zeros((xp.shape[0], CONV_W - 1, D_MODEL), xp.dtype)
            mp, st_p = short_conv_mixer(hp, zero_hist, a_w_in[j], a_conv_w[j], a_w_out[j])
            ms, st_s = short_conv_mixer(hs, cache_conv[j], a_w_in[j], a_conv_w[j], a_w_out[j])
            conv_prompt.append(st_p)
            conv_sample.append(st_s)
        else:
            mp, _ = chunk_sgu_mixer(hp, b_w_in[j], b_b_in[j], b_ln_g[j], b_ln_b[j],
                                    b_w_s[j], b_b_s[j], b_w_out[j])
            ms, v_s = chunk_sgu_mixer(hs, b_w_in[j], b_b_in[j], b_ln_g[j], b_ln_b[j],
                                      b_w_s[j], b_b_s[j], b_w_out[j])
            sgu_sample.append(v_s)
        xp = xp + rms_norm(mp, norm_mix_post[i])
        xs = xs + rms_norm(ms, norm_mix_post[i])
        xp = xp + rms_norm(swiglu_ffn(rms_norm(xp, norm_ffn_pre[i]), ffn_w_gate[i], ffn_w_up[i],
                                      ffn_w_down[i]), norm_ffn_post[i])
        xs = xs + rms_norm(swiglu_ffn(rms_norm(xs, norm_ffn_pre[i]), ffn_w_gate[i], ffn_w_up[i],
                                      ffn_w_down[i]), norm_ffn_post[i])
    state_conv_prompt = jnp.stack(conv_prompt)
    state_conv_sample = jnp.stack(conv_sample)
    state_sgu_v_sample = jnp.stack(sgu_sample)
    return (xp, xs, state_conv_prompt, state_conv_sample, state_sgu_v_sample)
```

```python
import os
import numpy as np
from contextlib import ExitStack
import concourse.bass as bass
import concourse.mybir as mybir
from concourse.bass_utils import run_bass_kernel_spmd

F32 = mybir.dt.float32
BF16 = mybir.dt.bfloat16
AF = mybir.ActivationFunctionType
ALU = mybir.AluOpType

D = 2048
DFF = 5632
T = 544
TX = 546
NPASS = 2
NCORES = 8
EPS = 1e-6
XROWS = 552

R_MIXPRE, R_MIXPOST, R_FFNPRE, R_FFNPOST = 0, 1, 2, 3
R_CONVW = 8
R_BU = 11


class Src:
    def __init__(self, name, inc):
        self.name, self.inc, self.count, self.sem = name, inc, 0, None


class Eng(Src):
    def __init__(self, name):
        super().__init__(name, 1)
        self.ops = []
        self.seen = {}

    def wait(self, toks):
        best = {}
        for s, c in toks:
            if c > best.get(s, 0):
                best[s] = c
        for s, c in best.items():
            if self.seen.get(s, 0) >= c:
                continue
            self.seen[s] = c
            self.ops.append(("w", s, c))

    def emit(self, fn, mark=True, slot=None):
        if slot is not None:
            slot.count += 1
            self.ops.append(("o", fn, slot))
            return (slot, slot.count)
        if mark:
            self.count += 1
            self.ops.append(("o", fn, self))
            return (self, self.count)
        self.ops.append(("o", fn, None))
        return None


class View:
    __slots__ = ("space", "lo", "hi", "ap", "ivals")

    def __init__(self, space, lo, hi, ap, ivals=None):
        self.space, self.lo, self.hi, self.ap = space, lo, hi, ap
        self.ivals = ivals if ivals is not None else [(lo, hi)]


class Tracker:
    def __init__(self):
        self.acc = {}

    def deps(self, v, kind, eng=None):
        out = []
        ps = v.space == "ps"
        for a in self.acc.get(v.space, ()):
            hit = False
            for lo, hi in v.ivals:
                if a[0] < hi and lo < a[1]:
                    hit = True
                    break
            if hit:
                if ps:
                    if a[3][0] is not eng:
                        out.append(a[3])
                elif kind == "w" or a[2] == "w":
                    out.append(a[3])
        return out

    def record(self, v, kind, tok):
        lst = self.acc.setdefault(v.space, [])
        for lo, hi in v.ivals:
            if kind == "w":
                lst[:] = [a for a in lst if not (lo <= a[0] and a[1] <= hi)]
            else:
                lst[:] = [a for a in lst if not (a[2] == "r" and a[0] == lo and a[1] == hi and a[3][0] is tok[0])]
            lst.append((lo, hi, kind, tok))


class Buf3:
    def __init__(self, space, base, C, N, esz, ap):
        self.space, self.base, self.C, self.N, self.esz, self.ap = space, base, C, N, esz, ap

    def v(self, c, a=0, b=None, c2=None):
        if b is None:
            b = self.N
        if c2 is None:
            lo = self.base + (c * self.N + a) * self.esz
            hi = self.base + (c * self.N + b) * self.esz
            return View(self.space, lo, hi, self.ap[:, c, a:b])
        lo = self.base + (c * self.N + a) * self.esz
        hi = self.base + ((c2 - 1) * self.N + b) * self.esz
        iv = None
        if (a, b) != (0, self.N):
            iv = [(self.base + (cc * self.N + a) * self.esz, self.base + (cc * self.N + b) * self.esz) for cc in range(c, c2)]
        return View(self.space, lo, hi, self.ap[:, c:c2, a:b], iv)


def two(ap2d):
    return ap2d.rearrange("p (b n) -> p b n", b=2)


def build_program(stop_after=None):
    nc = bass.Bass("TRN2", target_bir_lowering=False)
    es = ExitStack()

    def dram_in(name, shape):
        return nc.dram_tensor(name, list(shape), F32, kind="ExternalInput").ap()

    def dram_out(name, shape):
        return nc.dram_tensor(name, list(shape), F32, kind="ExternalOutput").ap()

    xin = dram_in("xin", [NPASS, XROWS, D])
    prm = dram_in("prm", [16, D])
    lnv = dram_in("lnv", [2, D])
    brow = dram_in("brow", [2, D])
    w_s = dram_in("w_s", [16, 128, 128])
    order = ["load", "conv", "ffn0", "sgu", "ffn1"]
    lvl = order.index(stop_after) if stop_after else 4
    a_w_in = dram_in("a_w_in", [D, 3 * D]) if lvl >= 1 else None
    a_w_out = dram_in("a_w_out", [D, D]) if lvl >= 1 else None
    b_w_in = dram_in("b_w_in", [D, 2 * D]) if lvl >= 3 else None
    b_w_out = dram_in("b_w_out", [D, D]) if lvl >= 3 else None
    w_gate = dram_in("w_gate", [2, D, DFF]) if lvl >= 2 else None
    w_up = dram_in("w_up", [2, D, DFF]) if lvl >= 2 else None
    w_down = dram_in("w_down", [2, DFF, D]) if lvl >= 2 else None
    declared = ["xin", "prm", "lnv", "brow", "w_s"] + (["a_w_in", "a_w_out"] if lvl >= 1 else []) + \
        (["w_gate", "w_up", "w_down"] if lvl >= 2 else []) + (["b_w_in", "b_w_out"] if lvl >= 3 else [])
    yout = dram_out("yout", [NPASS, T, D])
    stc_out = dram_out("stc", [NPASS, 6, D])
    sgu_out = dram_out("sgu", [NPASS, 32, D])

    XO = 0
    HO = XO + 16 * TX * 4
    R2O = HO + 16 * TX * 2
    R1O = R2O + 16 * T * 4
    RINGO = R1O + 65536
    NRING = 3
    ARB = RINGO + NRING * 8192
    AR = es.enter_context(nc.sbuf_tensor("arena", [128, ARB // 2], BF16))

    def a_view(lo, nbytes, dt):
        return AR[:, lo // 2:(lo + nbytes) // 2].bitcast(dt) if dt is not BF16 else AR[:, lo // 2:(lo + nbytes) // 2]

    def a_buf3(lo, C, N, dt):
        esz = 4 if dt is F32 else 2
        ap = a_view(lo, C * N * esz, dt).rearrange("p (c n) -> p c n", c=C)
        return Buf3("ar", lo, C, N, esz, ap)

    xfm = a_buf3(XO, 16, TX, F32)
    hbuf = a_buf3(HO, 16, TX, BF16)
    tmpb = a_buf3(R2O, 16, T, F32)
    vraw = a_buf3(R2O, 1, D, F32)
    vbf = a_buf3(R2O + 8192, 5, D, BF16)
    sgb = a_buf3(R2O, 2, T, F32)
    ybuf = a_buf3(R1O, 16, T, BF16)
    ctmp = a_buf3(R1O + 16 * T * 2, 6, 552, F32)
    ctmp7 = a_buf3(R1O + 16 * T * 2, 7, 552, F32)
    gbuf = a_buf3(R1O, 44, T, BF16)
    wvb = a_buf3(R1O, 16, D, BF16)
    stg = a_buf3(R1O, 5, D, F32)
    ring = [a_buf3(RINGO + i * 8192, 16, 256, BF16) for i in range(NRING)]

    def sb(name, shape, dt):
        return es.enter_context(nc.sbuf_tensor(name, list(shape), dt))

    ident = sb("ident", [128, 128], F32)
    ones_bf = sb("ones_bf", [128, 128], BF16)
    prm_fm = sb("prm_fm", [128, 16, 12], F32)
    lnb_t = sb("lnb_t", [128, 2, D], F32)
    rows = sb("rows", [128, D], BF16)
    rows_s = sb("rows_s", [2, 256], BF16)
    wst = sb("wst", [128, 16, 128], BF16)
    bd = sb("bd", [32, 16, 32], BF16)
    sq = sb("sq", [128, 2, TX], F32)
    acc = sb("acc", [128, TX], F32)
    ones_f = sb("ones_f", [128, 128], F32)
    hist = sb("hist", [128, 16, 4], F32)
    bnst = sb("bnst", [128, 4, 6], F32)
    bnag = sb("bnag", [128, 2], F32)
    rsd = sb("rsd", [128, 1], F32)
    epsb = sb("epsb", [128, 1], F32)
    ps = es.enter_context(nc.psum_tensor("ps", [128, 8, 512], F32))

    def sview(name, ap, nbytes=1 << 20):
        return View(name, 0, nbytes, ap)

    PE, ACT, DVE, POOL, SP = Eng("pe"), Eng("act"), Eng("dve"), Eng("pool"), Eng("sp")
    TR = Tracker()
    slots = []

    def new_slot(name):
        s = Src(f"{name}_{len(slots)}", 16)
        slots.append(s)
        return s

    def do(E, fn, reads=(), writes=(), mark=True, slot=None):
        toks = []
        for v in reads:
            toks += TR.deps(v, "r", E)
        for v in writes:
            toks += TR.deps(v, "w", E)
        E.wait(toks)
        tok = E.emit(fn, mark, slot)
        if tok is not None:
            for v in reads:
                TR.record(v, "r", tok)
            for v in writes:
                TR.record(v, "w", tok)
        return tok

    def psv(b0, b1, a=0, b=512):
        lo = b0 * 2048
        hi = b1 * 2048
        if b1 - b0 == 1:
            return View("ps", lo, hi, ps[:, b0, a:b])
        return View("ps", lo, hi, ps[:, b0:b1, a:b])

    pair_ctr = [0]

    def next_pair():
        p = pair_ctr[0] % 3
        pair_ctr[0] += 1
        return p

    ring_ctr = [0]
    extra = [a_buf3(R1O + 24576 + i * 8192, 16, 256, BF16) for i in range(5)]
    ring_all = [(ring[i], new_slot(f"ring{i}")) for i in range(NRING)] + [(extra[i], new_slot(f"xring{i}")) for i in range(5)]
    active = [ring_all[:NRING]]

    def set_ring(extras):
        active[0] = ring_all[:NRING] + [ring_all[NRING + i] for i in extras]

    def load_wblock(W2d, r0, nk, c0, ncols=256):
        lst = active[0]
        buf, slot = lst[ring_ctr[0] % len(lst)]
        ring_ctr[0] += 1
        src = W2d[r0:r0 + nk * 128, c0:c0 + ncols].rearrange("(k p) f -> p k f", p=128)
        v = buf.v(0, 0, ncols, c2=nk)
        do(POOL, lambda e, v=v, src=src: e.dma_start(out=v.ap, in_=src), writes=[v], slot=slot)
        return v

    identv = sview("ident", ident[:])
    onesv = sview("ones", ones_bf[:])
    prmv = sview("prm", prm_fm[:])
    lnbv = sview("lnb", lnb_t[:])
    rowsv = sview("rows", rows[:])
    rowssv = sview("rows_s", rows_s[:])
    wstv = sview("wst", wst[:])
    bdv = sview("bd", bd[:])
    histv = sview("hist", hist[:])
    bnstv = sview("bnst", bnst[:])
    bnagv = sview("bnag", bnag[:])
    rsdv = sview("rsd", rsd[:])
    epsv = sview("eps", epsb[:])

    def sqv(i, n):
        return View("sq", i * TX * 4, (i * TX + n) * 4, sq[:, i, 0:n])
    accv = sview("acc", acc[:])
    rstd, rstdv = acc, accv
    stcb = a_buf3(R2O + 32768, 16, 6, F32)
    stcf, stcfv = stcb.ap, stcb.v(0, c2=16)
    onesfv = sview("ones_f", ones_f[:])

    scr = a_buf3(R2O, 4, D, F32)

    def setup_early():
        do(POOL, lambda e: e.memset(ident[:], 0.0), writes=[identv])
        do(POOL, lambda e: e.affine_select(out=ident[:], in_=ident[:], compare_op=ALU.not_equal, fill=1.0,
                                           base=0, pattern=[[-1, 128]], channel_multiplier=1),
           reads=[identv], writes=[identv])
        do(POOL, lambda e: e.memset(ones_bf[:], 1.0), writes=[onesv])
        do(POOL, lambda e: e.memset(ones_f[:], 1.0), writes=[onesfv])
        do(POOL, lambda e: e.memset(epsb[:], EPS), writes=[epsv])

    def setup_prm():
        sc0 = stg.v(0)
        do(SP, lambda e: e.dma_start(out=stg.ap[0:16, 0, :], in_=prm), writes=[sc0], slot=new_slot("prm"))

        def fn_prm_tr(e):
            for c in range(16):
                ins = e.transpose(ps[:, 0, c * 16:(c + 1) * 16], stg.ap[0:16, 0, c * 128:(c + 1) * 128], ident[0:16, 0:16])
            return ins
        do(PE, fn_prm_tr, reads=[sc0, identv], writes=[psv(0, 1)])
        do(DVE, lambda e: e.tensor_copy(out=prm_fm[:], in_=ps[:, 0, 0:256].rearrange("p (c r) -> p c r", c=16)[:, :, 0:12]),
           reads=[psv(0, 1)], writes=[prmv])

    def setup_late_a():
        sc0, sc1, sc2 = scr.v(0), scr.v(1), scr.v(2)
        lv0 = View("lnb", 0, D * 4, lnb_t[:, 0, :])
        lv1 = View("lnb", D * 4, 2 * D * 4, lnb_t[:, 1, :])
        do(SP, lambda e: e.dma_start(out=lnb_t[:, 0, :], in_=lnv[0].partition_broadcast(128)), writes=[lv0], slot=new_slot("ln0"))
        do(SP, lambda e: e.dma_start(out=lnb_t[:, 1, :], in_=lnv[1].partition_broadcast(128)), writes=[lv1], slot=new_slot("ln1"))
        do(SP, lambda e: e.dma_start(out=scr.ap[0:2, 1, :], in_=brow), writes=[sc1], slot=new_slot("brow"))
        hi_bf = scr.ap[:, 2, 0:D // 2].bitcast(BF16)
        lo_bf = scr.ap[:, 2, D // 2:D].bitcast(BF16)
        h2 = View("ar", scr.base + 2 * D * 4, scr.base + 2 * D * 4 + D * 2, hi_bf)
        l2 = View("ar", scr.base + 2 * D * 4 + D * 2, scr.base + 3 * D * 4, lo_bf)
        do(DVE, lambda e: e.tensor_copy(out=hi_bf[0:2, :], in_=scr.ap[0:2, 1, :]), reads=[sc1], writes=[h2])
        do(DVE, lambda e: e.tensor_copy(out=scr.ap[0:2, 0, :], in_=hi_bf[0:2, :]), reads=[h2], writes=[sc0])
        do(DVE, lambda e: e.tensor_tensor(out=scr.ap[0:2, 0, :], in0=scr.ap[0:2, 1, :], in1=scr.ap[0:2, 0, :],
                                          op=ALU.subtract), reads=[sc1, sc0], writes=[sc0])
        do(DVE, lambda e: e.tensor_copy(out=lo_bf[0:2, :], in_=scr.ap[0:2, 0, :]), reads=[sc0], writes=[l2])
        for (dst, src, rv) in ((rows[0:1, :], hi_bf[0:1, :], h2), (rows[1:2, :], lo_bf[0:1, :], l2),
                               (rows[32:33, :], hi_bf[1:2, :], h2), (rows[33:34, :], lo_bf[1:2, :], l2)):
            do(SP, lambda e, dst=dst, src=src: e.dma_start(out=dst, in_=src), reads=[rv], writes=[rowsv],
               slot=new_slot("rows"))
        for (r, src, rv) in ((0, hi_bf, h2), (1, lo_bf, l2)):
            do(SP, lambda e, r=r, src=src: e.dma_start(
                out=rows_s[r:r + 1, :].rearrange("p (g i) -> p g i", g=16),
                in_=src[1:2, :].rearrange("p (g i) -> p g i", g=16)[:, :, 0:16]),
               reads=[rv], writes=[rowssv], slot=new_slot("rows_s"))
        wsf = scr.ap[:, 0, :].rearrange("p (g j) -> p g j", g=16)
        do(SP, lambda e: e.dma_start(out=wsf, in_=w_s.rearrange("g i j -> i g j")), writes=[sc0], slot=new_slot("ws"))
        s16 = scr.ap[0:16, 2, 0:512].rearrange("p (g k j) -> p g k j", g=16, k=2)
        for k in range(2):
            do(SP, lambda e, k=k: e.dma_start(out=s16[:, :, k, :], in_=w_s[:, 0:16, 0:16].rearrange("g i j -> i g j")),
               writes=[sc2], slot=new_slot("s16"))

    def setup_late_b():
        sc0, sc1, sc2 = scr.v(0), scr.v(1), scr.v(2)
        wsf = scr.ap[:, 0, :].rearrange("p (g j) -> p g j", g=16)
        wstf = scr.ap[:, 1, :].rearrange("p (g j) -> p g j", g=16)
        for half in range(2):
            def fn_ws_tr(e, half=half):
                for gg in range(8):
                    g = half * 8 + gg
                    ins = e.transpose(ps[:, 6 + gg // 4, (gg % 4) * 128:(gg % 4 + 1) * 128], wsf[:, g, :], ident[:])
                return ins
            do(PE, fn_ws_tr, reads=[sc0, identv], writes=[psv(6, 8)])
            do(ACT, lambda e, half=half: e.activation(out=scr.ap[:, 1, half * 1024:(half + 1) * 1024],
                                                      in_=ps[:, 6:8, :].rearrange("p b n -> p (b n)"), func=AF.Copy),
               reads=[psv(6, 8)], writes=[scr.v(1, half * 1024, (half + 1) * 1024)])
        do(POOL, lambda e: e.affine_select(out=wst[:], in_=wstf, compare_op=ALU.is_ge, fill=0.0, base=0,
                                           pattern=[[0, 16], [1, 128]], channel_multiplier=-1),
           reads=[sc1], writes=[wstv])

        def fn_bd_tr(e):
            for g in range(16):
                ins = e.transpose(ps[0:32, 6, g * 16:(g + 1) * 16],
                                  scr.ap[0:16, 2, g * 32:(g + 1) * 32], ident[0:16, 0:16])
            return ins
        do(PE, fn_bd_tr, reads=[sc2, identv], writes=[psv(6, 7)])
        bdf = scr.ap[0:32, 0, 0:512].rearrange("p (g q) -> p g q", g=16)
        t2 = ps[0:32, 6, 0:256].rearrange("p (g i) -> p g i", g=16)
        do(DVE, lambda e: e.tensor_copy(out=bdf[:, :, 0:16], in_=t2), reads=[psv(6, 7)], writes=[sc0])
        do(DVE, lambda e: e.tensor_copy(out=bdf[:, :, 16:32], in_=t2), reads=[psv(6, 7)], writes=[sc0])
        do(POOL, lambda e: e.affine_select(out=bdf, in_=bdf, compare_op=ALU.is_ge, fill=0.0, base=0,
                                           pattern=[[0, 16], [1, 32]], channel_multiplier=-1),
           reads=[sc0], writes=[sc0])
        do(POOL, lambda e: e.memset(bdf[0:16, :, 16:32], 0.0), reads=[sc0], writes=[sc0])
        do(DVE, lambda e: e.tensor_copy(out=bd[:], in_=bdf), reads=[sc0], writes=[bdv])

    def setup_late():
        setup_late_a()
        setup_late_b()

    def stats_chunk(c, src, n, first, last, on_pe=False):
        nb = n // 2
        s_ = sqv(c % 2, n)
        if len(src.ap.shape) == 3:
            do(ACT, lambda e: e.activation(out=two(s_.ap), in_=src.ap, func=AF.Square), reads=[src], writes=[s_])
        else:
            do(ACT, lambda e: e.activation(out=s_.ap, in_=src.ap, func=AF.Square), reads=[src], writes=[s_])
        if on_pe:
            def fn(e):
                for tb in range(2):
                    ins = e.matmul(ps[:, 6 + tb, 0:nb], lhsT=ones_f[:], rhs=s_.ap[:, tb * nb:(tb + 1) * nb],
                                   start=first, stop=last)
                return ins
            do(PE, fn, reads=[s_, onesfv], writes=[psv(6, 8, 0, nb)])
        elif first:
            do(DVE, lambda e: e.tensor_copy(out=acc[:, 0:n], in_=s_.ap), reads=[s_], writes=[accv])
        else:
            do(DVE, lambda e: e.tensor_tensor(out=acc[:, 0:n], in0=acc[:, 0:n], in1=s_.ap, op=ALU.add),
               reads=[s_, accv], writes=[accv])

    def stats_finish(n, on_pe=False):
        nb = n // 2
        if not on_pe:
            def fn(e):
                for tb in range(2):
                    ins = e.matmul(ps[:, 6 + tb, 0:nb], lhsT=ones_f[:], rhs=acc[:, tb * nb:(tb + 1) * nb], start=True, stop=True)
                return ins
            do(PE, fn, reads=[accv, onesfv], writes=[psv(6, 8, 0, nb)])
        do(ACT, lambda e: e.activation(out=two(rstd[:, 0:n]), in_=ps[:, 6:8, 0:nb], func=AF.Ln, scale=1.0 / D,
                                       bias=epsb[:, 0:1]), reads=[psv(6, 8, 0, nb), epsv], writes=[rstdv])
        do(ACT, lambda e: e.activation(out=rstd[:, 0:n], in_=rstd[:, 0:n], func=AF.Exp, scale=-0.5),
           reads=[rstdv], writes=[rstdv])

    def pre_norm(row, n):
        for c in range(16):
            stats_chunk(c, xfm.v(c, 0, n), n, c == 0, c == 15, on_pe=True)
        stats_finish(n, on_pe=True)
        for c in range(16):
            hv = hbuf.v(c, 0, n)
            xv = xfm.v(c, 0, n)
            do(DVE, lambda e, hv=hv, xv=xv, c=c: e.scalar_tensor_tensor(
                out=hv.ap, in0=xv.ap, scalar=prm_fm[:, c, row:row + 1], in1=rstd[:, 0:n], op0=ALU.mult, op1=ALU.mult),
               reads=[xv, prmv, rstdv], writes=[hv])

    def post_consume(c, pr):
        pv = psv(2 * pr, 2 * pr + 2, 0, T // 2)
        tv = tmpb.v(c)
        do(DVE, lambda e: e.tensor_copy(out=two(tv.ap), in_=pv.ap), reads=[pv], writes=[tv])
        stats_chunk(c, pv, T, c == 0, c == 15)

    def post_finish(row):
        stats_finish(T)

        def mul(c):
            tv = tmpb.v(c)
            do(DVE, lambda e: e.scalar_tensor_tensor(
                out=tv.ap, in0=tv.ap, scalar=prm_fm[:, c, row:row + 1], in1=rstd[:, 0:T], op0=ALU.mult, op1=ALU.mult),
               reads=[tv, prmv, rstdv], writes=[tv])

        def add(c):
            tv = tmpb.v(c)
            xv = xfm.v(c, 0, T)
            do(DVE, lambda e: e.tensor_tensor(out=xv.ap, in0=xv.ap, in1=tv.ap, op=ALU.add),
               reads=[tv, xv], writes=[xv])
        mul(0)
        for c in range(16):
            if c + 1 < 16:
                mul(c + 1)
            add(c)

    def linear_ws(W2d, K, c0_list, rhs, n, consume, progressive=False):
        nb = n // 2
        KC = K // 128
        subs = [(s, min(16, KC - s)) for s in range(0, KC, 16)]
        for bi, c0 in enumerate(c0_list):
            pairs = [next_pair(), next_pair()]
            for (s, nk) in subs:
                wv_ = load_wblock(W2d, s * 128, nk, c0)
                if bi == 0 and progressive and KC == 16:
                    for k in range(nk):
                        def fnk(e, wv_=wv_, k=k, pairs=tuple(pairs)):
                            for fc in range(2):
                                for tb in range(2):
                                    ins = e.matmul(ps[:, 2 * pairs[fc] + tb, 0:nb], lhsT=wv_.ap[:, k, fc * 128:(fc + 1) * 128],
                                                   rhs=rhs(k).ap[:, tb * nb:(tb + 1) * nb], start=(k == 0), stop=(k == KC - 1))
                            return ins
                        do(PE, fnk, reads=[wv_, rhs(k)],
                           writes=[psv(2 * pairs[0], 2 * pairs[0] + 2, 0, nb), psv(2 * pairs[1], 2 * pairs[1] + 2, 0, nb)])
                    continue
                for fc in range(2):
                    pr = pairs[fc]

                    def fn(e, wv_=wv_, fc=fc, pr=pr, s=s, nk=nk):
                        for tb in range(2):
                            for k in range(nk):
                                ins = e.matmul(ps[:, 2 * pr + tb, 0:nb], lhsT=wv_.ap[:, k, fc * 128:(fc + 1) * 128],
                                               rhs=rhs(s + k).ap[:, tb * nb:(tb + 1) * nb],
                                               start=(s + k == 0), stop=(s + k == KC - 1))
                        return ins
                    do(PE, fn, reads=[wv_] + [rhs(s + k) for k in range(nk)], writes=[psv(2 * pr, 2 * pr + 2, 0, nb)])
            for fc in range(2):
                consume(bi, fc, pairs[fc])

    stg_slots = [new_slot(f"stg{i}") for i in range(5)]
    out_slots = [new_slot(f"out{i}") for i in range(5)]
    st_slot = new_slot("stc")
    sg_slot = new_slot("sgu")
    wv_slots = [new_slot(f"wv{i}") for i in range(4)]

    lst = a_buf3(R2O, 4, D, F32)
    l4 = a_buf3(HO, 1, D, F32)

    def load_dma(p):
        for t in range(4):
            v = lst.v(t)
            do(SP, lambda e, t=t, v=v: e.dma_start(out=v.ap, in_=xin[p, t * 128:(t + 1) * 128, :]), writes=[v],
               slot=stg_slots[t])
        do(SP, lambda e: e.dma_start(out=l4.ap[0:40, 0, :], in_=xin[p, 512:552, :]), writes=[l4.v(0)], slot=stg_slots[4])

    def load_compute(p):
        v4 = l4.v(0)
        for c in range(16):
            bank = c % 6

            def fn(e, c=c, bank=bank):
                for t in range(4):
                    ins = e.transpose(ps[:, bank, t * 128:(t + 1) * 128], lst.ap[:, t, c * 128:(c + 1) * 128], ident[:])
                return ins
            do(PE, fn, reads=[lst.v(0, c2=4), identv], writes=[psv(bank, bank + 1)])
            xv = xfm.v(c, 0, 512)
            E = ACT if c % 2 == 0 else DVE
            if E is ACT:
                do(ACT, lambda e, xv=xv, bank=bank: e.activation(out=xv.ap, in_=ps[:, bank, :], func=AF.Copy),
                   reads=[psv(bank, bank + 1)], writes=[xv])
            else:
                do(DVE, lambda e, xv=xv, bank=bank: e.tensor_copy(out=xv.ap, in_=ps[:, bank, :]),
                   reads=[psv(bank, bank + 1)], writes=[xv])

        def fn4(e):
            for c in range(16):
                ins = e.transpose(ps[:, 6 + c // 8, (c % 8) * 40:(c % 8 + 1) * 40], l4.ap[0:40, 0, c * 128:(c + 1) * 128],
                                  ident[0:40, 0:40])
            return ins
        do(PE, fn4, reads=[v4, identv], writes=[psv(6, 8)])
        for hb in range(2):
            src = ps[:, 6 + hb, 0:320].rearrange("p (c r) -> p c r", c=8)
            c0 = hb * 8
            xs = xfm.v(c0, 512, 546, c2=c0 + 8)
            do(DVE, lambda e, src=src, c0=c0: e.tensor_copy(out=xfm.ap[:, c0:c0 + 8, 512:546], in_=src[:, :, 0:34]),
               reads=[psv(6 + hb, 7 + hb)], writes=[xs])
            do(DVE, lambda e, src=src, c0=c0: e.tensor_copy(out=hist[:, c0:c0 + 8, :], in_=src[:, :, 34:38]),
               reads=[psv(6 + hb, 7 + hb)], writes=[histv])

    def stage_store(p):
        for t in range(5):
            M = 128 if t < 4 else 32
            sv = stg.v(t)
            for cg in range(4):
                bank = (t * 4 + cg) % 6

                def fn(e, t=t, cg=cg, bank=bank, M=M):
                    for cc in range(4):
                        c = cg * 4 + cc
                        ins = e.transpose(ps[0:M, bank, cc * 128:(cc + 1) * 128], xfm.ap[:, c, t * 128:t * 128 + M], ident[:])
                    return ins
                do(PE, fn, reads=[xfm.v(cg * 4, t * 128, t * 128 + M, c2=cg * 4 + 4), identv], writes=[psv(bank, bank + 1)])
                E = ACT if cg % 2 == 0 else DVE
                if E is ACT:
                    do(ACT, lambda e, t=t, cg=cg, bank=bank, M=M: e.activation(
                        out=stg.ap[0:M, t, cg * 512:(cg + 1) * 512], in_=ps[0:M, bank, :], func=AF.Copy),
                       reads=[psv(bank, bank + 1)], writes=[sv])
                else:
                    do(DVE, lambda e, t=t, cg=cg, bank=bank, M=M: e.tensor_copy(
                        out=stg.ap[0:M, t, cg * 512:(cg + 1) * 512], in_=ps[0:M, bank, :]),
                       reads=[psv(bank, bank + 1)], writes=[sv])
            do(SP, lambda e, t=t, M=M: e.dma_start(out=yout[p, t * 128:t * 128 + M, :], in_=stg.ap[0:M, t, :]),
               reads=[sv], slot=out_slots[t])

    def mixer_out(W2d, layer):
        def consume(bi, fc, pr):
            post_consume(2 * bi + fc, pr)
        linear_ws(W2d, D, [256 * q for q in range(8)], lambda k: ybuf.v(k), T, consume)
        post_finish(R_MIXPOST + 4 * layer)

    def stage_conv(p, hook=None):
        CL = int(os.environ.get("MK_CONV_LVL") or 9)
        set_ring([2, 3, 4])
        for c in range(16):
            xv = xfm.v(c, 0, TX)
            hv = hbuf.v(c, 0, TX)
            do(ACT, lambda e, xv=xv, hv=hv, c=c: e.activation(out=hv.ap, in_=xv.ap, func=AF.Copy,
                                                              scale=prm_fm[:, c, R_MIXPRE:R_MIXPRE + 1]),
               reads=[xv, prmv], writes=[hv])
            stats_chunk(c, xv, TX, c == 0, c == 15, on_pe=True)
        stats_finish(TX, on_pe=True)
        r2v = ctmp7.v(6, 0, TX)
        r2 = ctmp7.ap[:, 6, 0:TX]
        do(DVE, lambda e: e.tensor_tensor(out=r2, in0=rstd[:, 0:TX], in1=rstd[:, 0:TX], op=ALU.mult),
           reads=[rstdv], writes=[r2v])
        if CL <= 1:
            return
        nbx = TX // 2
        GC, CZ, CV = 0, 2, 4

        def consume(bi, fc, pr):
            kind, q = bi % 3, bi // 3
            c = 2 * q + fc
            pv = psv(2 * pr, 2 * pr + 2, 0, nbx)
            if CL <= 2:
                gv = ctmp.v(GC + fc, 0, TX)
                do(ACT, lambda e: e.activation(out=two(gv.ap), in_=pv.ap, func=AF.Copy), reads=[pv], writes=[gv])
                return
            if kind == 0:
                gv = ctmp.v(GC + fc, 0, TX)
                do(ACT, lambda e: e.activation(out=two(gv.ap), in_=pv.ap, func=AF.Copy), reads=[pv], writes=[gv])
            elif kind == 1:
                gv = ctmp.v(GC + fc, 0, TX)
                cz = ctmp.v(CZ + fc, 0, 550)
                gsb = ctmp.ap[:, GC + fc, :]
                czb = ctmp.ap[:, CZ + fc, :]
                zrv = ctmp.v(CV + fc, 0, TX)
                zr = ctmp.ap[:, CV + fc, :]
                do(DVE, lambda e: e.tensor_tensor(out=two(zrv.ap), in0=pv.ap, in1=two(r2), op=ALU.mult),
                   reads=[pv, r2v], writes=[zrv])
                do(DVE, lambda e: e.tensor_tensor(out=czb[:, 2:514], in0=zr[:, 0:512], in1=gsb[:, 0:512],
                                                  op=ALU.mult), reads=[zrv, gv], writes=[cz])
                do(DVE, lambda e: e.tensor_tensor(
                    out=czb[:, 514:550].rearrange("p (s r) -> p s r", s=2)[:, :, 2:18],
                    in0=zr[:, 512:544].rearrange("p (s r) -> p s r", s=2),
                    in1=gsb[:, 512:544].rearrange("p (s r) -> p s r", s=2), op=ALU.mult),
                   reads=[zrv, gv], writes=[cz])
                do(DVE, lambda e: e.tensor_tensor(out=czb[:, 0:2], in0=zr[:, 544:546], in1=gsb[:, 544:546],
                                                  op=ALU.mult), reads=[zrv, gv], writes=[cz])
                do(DVE, lambda e: e.tensor_copy(
                    out=czb[:, 514:550].rearrange("p (s r) -> p s r", s=2)[:, :, 0:2],
                    in_=hist[:, c, :].rearrange("p (s r) -> p s r", s=2)), reads=[histv], writes=[cz])
                do(DVE, lambda e: e.tensor_copy(
                    out=stcf[:, c, :].rearrange("p (s r) -> p s r", s=3),
                    in_=czb[:, 496:550].rearrange("p (s r) -> p s r", s=3)[:, :, 16:18]), reads=[cz], writes=[stcfv])
                cv = ctmp.v(CV + fc, 0, 548)
                cvb = ctmp.ap[:, CV + fc, :]
                w0 = prm_fm[:, c, R_CONVW:R_CONVW + 1]
                w1 = prm_fm[:, c, R_CONVW + 1:R_CONVW + 2]
                w2 = prm_fm[:, c, R_CONVW + 2:R_CONVW + 3]
                do(DVE, lambda e: e.tensor_scalar(out=cvb[:, 0:548], in0=czb[:, 0:548], scalar1=w0, scalar2=None,
                                                  op0=ALU.mult), reads=[cz, prmv], writes=[cv])
                do(DVE, lambda e: e.scalar_tensor_tensor(out=cvb[:, 0:548], in0=czb[:, 1:549], scalar=w1, in1=cvb[:, 0:548],
                                                         op0=ALU.mult, op1=ALU.add), reads=[cz, cv, prmv], writes=[cv])
                do(DVE, lambda e: e.scalar_tensor_tensor(out=cvb[:, 0:548], in0=czb[:, 2:550], scalar=w2, in1=cvb[:, 0:548],
                                                         op0=ALU.mult, op1=ALU.add), reads=[cz, cv, prmv], writes=[cv])
            else:
                cv = ctmp.v(CV + fc, 0, 548)
                cvb = ctmp.ap[:, CV + fc, :]
                yv = ybuf.v(c)
                yb = ybuf.ap[:, c, :]
                gbv = ctmp.v(GC + fc, 0, TX)
                gb = ctmp.ap[:, GC + fc, :]
                do(DVE, lambda e: e.tensor_tensor(out=two(gbv.ap), in0=pv.ap, in1=two(rstd[:, 0:TX]), op=ALU.mult),
                   reads=[pv, rstdv], writes=[gbv])
                do(DVE, lambda e: e.tensor_tensor(out=yb[:, 0:512], in0=gb[:, 0:512], in1=cvb[:, 0:512], op=ALU.mult),
                   reads=[gbv, cv], writes=[yv])
                do(DVE, lambda e: e.tensor_tensor(
                    out=yb[:, 512:544].rearrange("p (s r) -> p s r", s=2),
                    in0=gb[:, 512:544].rearrange("p (s r) -> p s r", s=2),
                    in1=cvb[:, 514:550].rearrange("p (s r) -> p s r", s=2)[:, :, 0:16], op=ALU.mult),
                   reads=[gbv, cv], writes=[yv])

        c0s = []
        for q in range(8):
            c0s += [D + 256 * q, 2 * D + 256 * q, 256 * q]
        linear_ws(a_w_in, D, c0s, lambda k: hbuf.v(k, 0, TX), TX, consume, progressive=True)
        if hook is not None:
            hook()
        if CL <= 3:
            return

        for half in range(2):
            def fn_h(e, half=half):
                for cc in range(8):
                    c = half * 8 + cc
                    ins = e.transpose(ps[0:6, 6 + cc // 4, (cc % 4) * 128:(cc % 4 + 1) * 128], stcf[:, c, :], ident[:])
                return ins
            do(PE, fn_h, reads=[stcfv, identv], writes=[psv(6, 8)])
            sv = stg.v(4, half * 1024, half * 1024 + 1024)
            do(ACT, lambda e, half=half: e.activation(
                out=stg.ap[0:6, 4, half * 1024:(half + 1) * 1024],
                in_=ps[0:6, 6:8, :].rearrange("p b n -> p (b n)"), func=AF.Copy), reads=[psv(6, 8)], writes=[sv])
        do(SP, lambda e: e.dma_start(out=stc_out[p], in_=stg.ap[0:6, 4, :]), reads=[stg.v(4)], slot=st_slot)
        if CL <= 4:
            return
        if CL <= 6:
            def consume_dbg(bi, fc, pr):
                c = 2 * bi + fc
                pv = psv(2 * pr, 2 * pr + 2, 0, T // 2)
                tv = tmpb.v(c)
                do(DVE, lambda e: e.tensor_copy(out=two(tv.ap), in_=pv.ap), reads=[pv], writes=[tv])
                if CL == 6:
                    stats_chunk(c, pv, T, c == 0, c == 15)
            linear_ws(a_w_out, D, [256 * q for q in range(8)], lambda k: ybuf.v(k), T, consume_dbg)
            if CL == 6:
                stats_finish(T)
            return
        mixer_out(a_w_out, 0)

    def stage_ffn(p, layer, hook=None):
        set_ring([3, 4])
        row = R_FFNPRE + 4 * layer
        for c in range(16):
            xv = xfm.v(c, 0, T)
            hv = hbuf.v(c, 0, T)
            do(ACT, lambda e, xv=xv, hv=hv, c=c: e.activation(out=hv.ap, in_=xv.ap, func=AF.Copy,
                                                              scale=prm_fm[:, c, row:row + 1]),
               reads=[xv, prmv], writes=[hv])
            stats_chunk(c, xv, T, c == 0, c == 15, on_pe=True)
        stats_finish(T, on_pe=True)
        hb = T // 2
        sgb4 = a_buf3(R2O + 24576, 4, T, F32)

        def consume(bi, fc, pr):
            kind, q = bi % 2, bi // 2
            f = 2 * q + fc
            pv = psv(2 * pr, 2 * pr + 2, 0, hb)
            sv = sgb4.v(fc)
            if kind == 0:
                do(DVE, lambda e: e.tensor_tensor(out=two(sv.ap), in0=pv.ap, in1=two(rstd[:, 0:T]), op=ALU.mult),
                   reads=[pv, rstdv], writes=[sv])
                do(ACT, lambda e: e.activation(out=sv.ap, in_=sv.ap, func=AF.Silu), reads=[sv], writes=[sv])
            else:
                gv = gbuf.v(f)
                uv = sgb4.v(2 + fc)
                do(DVE, lambda e: e.tensor_tensor(out=two(uv.ap), in0=pv.ap, in1=two(rstd[:, 0:T]), op=ALU.mult),
                   reads=[pv, rstdv], writes=[uv])
                do(DVE, lambda e: e.tensor_tensor(out=gv.ap, in0=uv.ap, in1=sv.ap, op=ALU.mult),
                   reads=[uv, sv], writes=[gv])

        c0s = []
        for q in range(22):
            c0s += [("g", 256 * q), ("u", 256 * q)]
        nb = hb
        for bi, (which, c0) in enumerate(c0s):
            W2d = w_gate[layer] if which == "g" else w_up[layer]
            linear_ws(W2d, D, [c0], lambda k: hbuf.v(k, 0, T), T, lambda _b, fc, pr, bi=bi: consume(bi, fc, pr),
                      progressive=(bi == 0))
            if hook is not None and bi == 1:
                setup_late_a()
            if hook is not None and bi == 33:
                setup_late_b()

        def consume_d(bi, fc, pr):
            post_consume(2 * bi + fc, pr)
        linear_ws(w_down[layer], DFF, [256 * q for q in range(8)], lambda k: gbuf.v(k), T, consume_d)
        post_finish(R_FFNPOST + 4 * layer)

    def stage_sgu(p):
        set_ring([])
        pre_norm(R_MIXPRE + 4, T)
        for n in range(4):
            v = wvb.v(0, n * 512, (n + 1) * 512, c2=16)
            src = b_w_in[:, D + n * 512:D + (n + 1) * 512].rearrange("(k p) f -> p k f", p=128)
            do(POOL, lambda e, v=v, src=src: e.dma_start(out=v.ap, in_=src), writes=[v], slot=wv_slots[n])
        def v_group(m, n):
            M = 128 if m < 4 else 32
            t0 = m * 128
            qb = 4 * (m % 2)

            def fn(e):
                for k in range(16):
                    e.matmul(ps[0:M, qb + n, :], lhsT=hbuf.ap[:, k, t0:t0 + M], rhs=wvb.ap[:, k, n * 512:(n + 1) * 512],
                             start=(k == 0), stop=False)
                ins = e.matmul(ps[0:M, qb + n, :], lhsT=ones_bf[0:2, 0:M], rhs=rows[0:2, n * 512:(n + 1) * 512],
                               start=False, stop=True)
                return ins
            do(PE, fn, reads=[hbuf.v(0, t0, t0 + M, c2=16), wvb.v(0, n * 512, (n + 1) * 512, c2=16), onesv, rowsv],
               writes=[psv(qb + n, qb + n + 1)])

        def v_post(m):
            M = 128 if m < 4 else 32
            qb = 4 * (m % 2)
            pq = psv(qb, qb + 4)
            vr = vraw.v(0)
            vra = vraw.ap[0:M, 0, :]
            do(ACT, lambda e: e.activation(out=vra, in_=ps[0:M, qb:qb + 4, :].rearrange("p b n -> p (b n)"), func=AF.Gelu),
               reads=[pq], writes=[vr])
            for n in range(4):
                do(DVE, lambda e, n=n: e.bn_stats(out=bnst[0:M, n, :], in_=vra[:, n * 512:(n + 1) * 512]),
                   reads=[vr], writes=[bnstv])
            do(DVE, lambda e: e.bn_aggr(out=bnag[0:M, :], in_=bnst[0:M, :, :]), reads=[bnstv], writes=[bnagv])
            do(ACT, lambda e: e.activation(out=rsd[0:M, :], in_=bnag[0:M, 1:2], func=AF.Sqrt, scale=1.0,
                                           bias=epsb[0:M, 0:1]), reads=[bnagv, epsv], writes=[rsdv])
            do(DVE, lambda e: e.reciprocal(out=rsd[0:M, :], in_=rsd[0:M, :]), reads=[rsdv], writes=[rsdv])
            do(DVE, lambda e: e.scalar_tensor_tensor(out=vra, in0=vra, scalar=bnag[0:M, 0:1], in1=lnb_t[0:M, 0, :],
                                                     op0=ALU.subtract, op1=ALU.mult), reads=[vr, bnagv, lnbv], writes=[vr])
            vb = vbf.v(m)
            if m < 4:
                do(DVE, lambda e: e.scalar_tensor_tensor(out=vbf.ap[0:M, m, :], in0=vra, scalar=rsd[0:M, :],
                                                         in1=lnb_t[0:M, 1, :], op0=ALU.mult, op1=ALU.add),
                   reads=[vr, rsdv, lnbv], writes=[vb])
            else:
                do(DVE, lambda e: e.scalar_tensor_tensor(out=vra, in0=vra, scalar=rsd[0:M, :], in1=lnb_t[0:M, 1, :],
                                                         op0=ALU.mult, op1=ALU.add), reads=[vr, rsdv, lnbv], writes=[vr])
                do(ACT, lambda e: e.activation(out=vbf.ap[0:M, m, :], in_=vra, func=AF.Copy), reads=[vr], writes=[vb])
                do(SP, lambda e: e.dma_start(out=sgu_out[p], in_=vra), reads=[vr], slot=sg_slot)

        for k in range(16):
            def fnk(e, k=k):
                for m in (0, 1):
                    ins = e.matmul(ps[:, 4 * m, :], lhsT=hbuf.ap[:, k, m * 128:(m + 1) * 128], rhs=wvb.ap[:, k, 0:512],
                                   start=(k == 0), stop=False)
                return ins
            do(PE, fnk, reads=[hbuf.v(k, 0, 256), wvb.v(k, 0, 512)], writes=[psv(0, 1), psv(4, 5)])

        def fnb(e):
            for m in (0, 1):
                ins = e.matmul(ps[:, 4 * m, :], lhsT=ones_bf[0:2, 0:128], rhs=rows[0:2, 0:512], start=False, stop=True)
            return ins
        do(PE, fnb, reads=[onesv, rowsv], writes=[psv(0, 1), psv(4, 5)])
        for n in range(1, 4):
            v_group(0, n)
            v_group(1, n)
        v_post(0)
        v_post(1)
        for m in (2, 3, 4):
            for n in range(4):
                v_group(m, n)
            v_post(m)
        hb = T // 2

        def consume_u(bi, fc, pr):
            c = 2 * bi + fc
            pv = psv(2 * pr, 2 * pr + 2, 0, hb)
            uv = ybuf.v(c)
            do(ACT, lambda e: e.activation(out=two(uv.ap), in_=pv.ap, func=AF.Gelu, bias=prm_fm[:, c, R_BU:R_BU + 1]),
               reads=[pv, prmv], writes=[uv])
        linear_ws(b_w_in, D, [256 * q for q in range(8)], lambda k: hbuf.v(k, 0, T), T, consume_u)
        for g in range(16):
            pr = next_pair()
            pm = psv(2 * pr, 2 * pr + 2)

            def fn(e, g=g, pr=pr):
                for m in range(4):
                    o = ps[:, 2 * pr, m * 128:(m + 1) * 128]
                    e.matmul(o, lhsT=vbf.ap[:, m, g * 128:(g + 1) * 128], rhs=wst[:, g, :], start=True, stop=False)
                    e.matmul(o, lhsT=ones_bf[32:34, :], rhs=rows[32:34, g * 128:(g + 1) * 128], start=False, stop=True)
                o = ps[:, 2 * pr + 1, 0:32]
                e.matmul(o, lhsT=vbf.ap[0:32, 4, g * 128:(g + 1) * 128], rhs=bd[:, g, :], start=True, stop=False)
                for s in range(2):
                    os_ = ps[:, 2 * pr + 1, s * 16:(s + 1) * 16]
                    ins = e.matmul(os_, lhsT=ones_bf[0:2, :], rhs=rows_s[0:2, g * 16:(g + 1) * 16], start=False,
                                   stop=(s == 1))
                return ins
            do(PE, fn, reads=[vbf.v(0, c2=5), wstv, bdv, onesv, rowsv, rowssv], writes=[pm])
            yv = ybuf.v(g)
            yb = ybuf.ap[:, g, :]
            do(DVE, lambda e, yb=yb, pr=pr: e.tensor_tensor(out=yb[:, 0:512], in0=ps[:, 2 * pr, :], in1=yb[:, 0:512],
                                                            op=ALU.mult), reads=[pm, yv], writes=[yv])
            do(DVE, lambda e, yb=yb, pr=pr: e.tensor_tensor(out=yb[:, 512:544], in0=ps[:, 2 * pr + 1, 0:32],
                                                            in1=yb[:, 512:544], op=ALU.mult), reads=[pm, yv], writes=[yv])
        set_ring([0, 1, 2, 3, 4])
        mixer_out(b_w_out, 1)

    stages = ["load", "conv", "ffn0", "sgu", "ffn1"]
    setup_early()
    setup_prm()
    load_dma(0)
    for p in range(NPASS):
        load_compute(p)
        if p == 0 and stop_after == "load":
            setup_late()
        if stop_after != "load":
            stage_conv(p)
            if stop_after != "conv":
                stage_ffn(p, 0, True if p == 0 else None)
                if stop_after != "ffn0":
                    stage_sgu(p)
                    if stop_after != "sgu":
                        stage_ffn(p, 1)
        if p + 1 < NPASS:
            load_dma(p + 1)
        stage_store(p)

    final = [(s, s.count) for s in out_slots + [st_slot, sg_slot] if s.count > 0]
    SP.wait(final)

    for s in [PE, ACT, DVE, POOL, SP] + slots:
        s.sem = es.enter_context(nc.semaphore("s_" + s.name))
    block = es.enter_context(nc.Block())

    def replay(E):
        def run(e):
            for op in E.ops:
                if op[0] == "w":
                    e.wait_ge(op[1].sem, op[2] * op[1].inc)
                else:
                    ins = op[1](e)
                    if op[2] is not None:
                        ins.then_inc(op[2].sem, op[2].inc)
        return run

    block.tensor(replay(PE))
    block.scalar(replay(ACT))
    block.vector(replay(DVE))
    block.gpsimd(replay(POOL))
    block.sync(replay(SP))
    es.close()
    return nc, declared


def make_in_maps(inp):
    f = lambda a: np.ascontiguousarray(np.asarray(a, dtype=np.float32))
    x_prompt, x_sample, cache_conv = f(inp["x_prompt"]), f(inp["x_sample"]), f(inp["cache_conv"])
    prm = np.zeros((16, D), np.float32)
    for l in range(2):
        prm[4 * l + 0] = inp["norm_mix_pre"][l]
        prm[4 * l + 1] = inp["norm_mix_post"][l]
        prm[4 * l + 2] = inp["norm_ffn_pre"][l]
        prm[4 * l + 3] = inp["norm_ffn_post"][l]
    prm[8:11] = inp["a_conv_w"][0]
    prm[11] = inp["b_b_in"][0][:D]
    shared = {
        "prm": prm,
        "lnv": f(np.stack([inp["b_ln_g"][0], inp["b_ln_b"][0]])),
        "brow": f(np.stack([inp["b_b_in"][0][D:], inp["b_b_s"][0].reshape(D)])),
        "w_s": f(inp["b_w_s"][0]),
        "a_w_in": f(inp["a_w_in"][0]),
        "a_w_out": f(inp["a_w_out"][0]),
        "b_w_in": f(inp["b_w_in"][0]),
        "b_w_out": f(inp["b_w_out"][0]),
        "w_gate": f(inp["ffn_w_gate"]),
        "w_up": f(inp["ffn_w_up"]),
        "w_down": f(inp["ffn_w_down"]),
    }
    maps = []
    for c in range(NCORES):
        xin = np.zeros((NPASS, XROWS, D), np.float32)
        for p in range(NPASS):
            vc = 2 * c + p
            b, s0 = vc // 4, (vc % 4) * 512
            xin[p, 0:512] = x_prompt[b, s0:s0 + 512]
            xin[p, 512:544] = x_sample[2 * vc:2 * vc + 2].reshape(32, D)
            if s0 > 0:
                xin[p, 544:546] = x_prompt[b, s0 - 2:s0]
            xin[p, 546:550] = cache_conv[0, 2 * vc:2 * vc + 2].reshape(4, D)
        m = dict(shared)
        m["xin"] = xin
        maps.append(m)
    return maps


def assemble(results):
    y_prompt = np.zeros((4, 2048, D), np.float32)
    y_sample = np.zeros((32, 16, D), np.float32)
    st_p = np.zeros((1, 4, 2, D), np.float32)
    st_s = np.zeros((1, 32, 2, D), np.float32)
    sgu = np.zeros((1, 32, 16, D), np.float32)
    for c in range(NCORES):
        r = results[c]
        for p in range(NPASS):
            vc = 2 * c + p
            b, s0 = vc // 4, (vc % 4) * 512
            y_prompt[b, s0:s0 + 512] = r["yout"][p, 0:512]
            y_sample[2 * vc:2 * vc + 2] = r["yout"][p, 512:544].reshape(2, 16, D)
            if vc % 4 == 3:
                st_p[0, b] = r["stc"][p, 0:2]
            st_s[0, 2 * vc:2 * vc + 2] = r["stc"][p, 2:6].reshape(2, 2, D)
            sgu[0, 2 * vc:2 * vc + 2] = r["sgu"][p].reshape(2, 16, D)
    return (y_prompt, y_sample, st_p, st_s, sgu)


def kernel(**inputs):
    nc, declared = build_program(os.environ.get("MK_STOP_AFTER") or None)
    maps = make_in_maps(inputs)
    maps = [{k: m[k] for k in declared} for m in maps]
    ncores = int(os.environ.get("MK_NCORES") or NCORES)
    if os.environ.get("MK_TRACE"):
        res = run_bass_kernel_spmd(nc, maps[:ncores], core_ids=list(range(ncores)), trace=True)
        print("MK_TRACE exec_time_ns", res.exec_time_ns)
    else:
        res = run_bass_kernel_spmd(nc, maps[:ncores], core_ids=list(range(ncores)))
    results = list(res.results) + [res.results[0]] * (NCORES - ncores)
    return assemble(results)
```

```python
import os
import numpy as np
from contextlib import ExitStack
import concourse.bass as bass
import concourse.mybir as mybir
from concourse.bass_utils import run_bass_kernel_spmd

F32 = mybir.dt.float32
BF16 = mybir.dt.bfloat16
AF = mybir.ActivationFunctionType
ALU = mybir.AluOpType

D = 2048
DFF = 5632
T = 544
TX = 546
NPASS = 2
NCORES = 8
EPS = 1e-6
XROWS = 552

R_MIXPRE, R_MIXPOST, R_FFNPRE, R_FFNPOST = 0, 1, 2, 3
R_CONVW = 8
R_BU = 11


class Src:
    def __init__(self, name, inc):
        self.name, self.inc, self.count, self.sem = name, inc, 0, None


class Eng(Src):
    def __init__(self, name):
        super().__init__(name, 1)
        self.ops = []
        self.seen = {}

    def wait(self, toks):
        best = {}
        for s, c in toks:
            if c > best.get(s, 0):
                best[s] = c
        for s, c in best.items():
            if self.seen.get(s, 0) >= c:
                continue
            self.seen[s] = c
            self.ops.append(("w", s, c))

    def emit(self, fn, mark=True, slot=None):
        if slot is not None:
            slot.count += 1
            self.ops.append(("o", fn, slot))
            return (slot, slot.count)
        if mark:
            self.count += 1
            self.ops.append(("o", fn, self))
            return (self, self.count)
        self.ops.append(("o", fn, None))
        return None


class View:
    __slots__ = ("space", "lo", "hi", "ap", "ivals")

    def __init__(self, space, lo, hi, ap, ivals=None):
        self.space, self.lo, self.hi, self.ap = space, lo, hi, ap
        self.ivals = ivals if ivals is not None else [(lo, hi)]


class Tracker:
    def __init__(self):
        self.acc = {}

    def deps(self, v, kind, eng=None):
        out = []
        ps = v.space == "ps"
        for a in self.acc.get(v.space, ()):
            hit = False
            for lo, hi in v.ivals:
                if a[0] < hi and lo < a[1]:
                    hit = True
                    break
            if hit:
                if ps:
                    if a[3][0] is not eng:
                        out.append(a[3])
                elif kind == "w" or a[2] == "w":
                    out.append(a[3])
        return out

    def record(self, v, kind, tok):
        lst = self.acc.setdefault(v.space, [])
        for lo, hi in v.ivals:
            if kind == "w":
                lst[:] = [a for a in lst if not (lo <= a[0] and a[1] <= hi)]
            else:
                lst[:] = [a for a in lst if not (a[2] == "r" and a[0] == lo and a[1] == hi and a[3][0] is tok[0])]
            lst.append((lo, hi, kind, tok))


class Buf3:
    def __init__(self, space, base, C, N, esz, ap):
        self.space, self.base, self.C, self.N, self.esz, self.ap = space, base, C, N, esz, ap

    def v(self, c, a=0, b=None, c2=None):
        if b is None:
            b = self.N
        if c2 is None:
            lo = self.base + (c * self.N + a) * self.esz
            hi = self.base + (c * self.N + b) * self.esz
            return View(self.space, lo, hi, self.ap[:, c, a:b])
        lo = self.base + (c * self.N + a) * self.esz
        hi = self.base + ((c2 - 1) * self.N + b) * self.esz
        iv = None
        if (a, b) != (0, self.N):
            iv = [(self.base + (cc * self.N + a) * self.esz, self.base + (cc * self.N + b) * self.esz) for cc in range(c, c2)]
        return View(self.space, lo, hi, self.ap[:, c:c2, a:b], iv)


def two(ap2d):
    return ap2d.rearrange("p (b n) -> p b n", b=2)


def build_program(stop_after=None):
    nc = bass.Bass("TRN2", target_bir_lowering=False)
    es = ExitStack()

    def dram_in(name, shape):
        return nc.dram_tensor(name, list(shape), F32, kind="ExternalInput").ap()

    def dram_out(name, shape):
        return nc.dram_tensor(name, list(shape), F32, kind="ExternalOutput").ap()

    xin = dram_in("xin", [NPASS, XROWS, D])
    prm = dram_in("prm", [16, D])
    lnv = dram_in("lnv", [2, D])
    brow = dram_in("brow", [2, D])
    w_s = dram_in("w_s", [16, 128, 128])
    order = ["load", "conv", "ffn0", "sgu", "ffn1"]
    lvl = order.index(stop_after) if stop_after else 4
    a_w_in = dram_in("a_w_in", [D, 3 * D]) if lvl >= 1 else None
    a_w_out = dram_in("a_w_out", [D, D]) if lvl >= 1 else None
    b_w_in = dram_in("b_w_in", [D, 2 * D]) if lvl >= 3 else None
    b_w_out = dram_in("b_w_out", [D, D]) if lvl >= 3 else None
    w_gate = dram_in("w_gate", [2, D, DFF]) if lvl >= 2 else None
    w_up = dram_in("w_up", [2, D, DFF]) if lvl >= 2 else None
    w_down = dram_in("w_down", [2, DFF, D]) if lvl >= 2 else None
    declared = ["xin", "prm", "lnv", "brow", "w_s"] + (["a_w_in", "a_w_out"] if lvl >= 1 else []) + \
        (["w_gate", "w_up", "w_down"] if lvl >= 2 else []) + (["b_w_in", "b_w_out"] if lvl >= 3 else [])
    yout = dram_out("yout", [NPASS, T, D])
    stc_out = dram_out("stc", [NPASS, 6, D])
    sgu_out = dram_out("sgu", [NPASS, 32, D])

    XO = 0
    HO = XO + 16 * TX * 4
    R2O = HO + 16 * TX * 2
    R1O = R2O + 16 * T * 4
    RINGO = R1O + 65536
    NRING = 3
    ARB = RINGO + NRING * 8192
    AR = es.enter_context(nc.sbuf_tensor("arena", [128, ARB // 2], BF16))

    def a_view(lo, nbytes, dt):
        return AR[:, lo // 2:(lo + nbytes) // 2].bitcast(dt) if dt is not BF16 else AR[:, lo // 2:(lo + nbytes) // 2]

    def a_buf3(lo, C, N, dt):
        esz = 4 if dt is F32 else 2
        ap = a_view(lo, C * N * esz, dt).rearrange("p (c n) -> p c n", c=C)
        return Buf3("ar", lo, C, N, esz, ap)

    xfm = a_buf3(XO, 16, TX, F32)
    hbuf = a_buf3(HO, 16, TX, BF16)
    tmpb = a_buf3(R2O, 16, T, F32)
    vraw = a_buf3(R2O, 1, D, F32)
    vbf = a_buf3(R2O + 8192, 5, D, BF16)
    sgb = a_buf3(R2O, 2, T, F32)
    ybuf = a_buf3(R1O, 16, T, BF16)
    ctmp = a_buf3(R1O + 16 * T * 2, 6, 552, F32)
    ctmp7 = a_buf3(R1O + 16 * T * 2, 7, 552, F32)
    gbuf = a_buf3(R1O, 44, T, BF16)
    wvb = a_buf3(R1O, 16, D, BF16)
    stg = a_buf3(R1O, 5, D, F32)
    ring = [a_buf3(RINGO + i * 8192, 16, 256, BF16) for i in range(NRING)]

    def sb(name, shape, dt):
        return es.enter_context(nc.sbuf_tensor(name, list(shape), dt))

    ident = sb("ident", [128, 128], F32)
    ones_bf = sb("ones_bf", [128, 128], BF16)
    prm_fm = sb("prm_fm", [128, 16, 12], F32)
    lnb_t = sb("lnb_t", [128, 2, D], F32)
    rows = sb("rows", [128, D], BF16)
    rows_s = sb("rows_s", [2, 256], BF16)
    wst = sb("wst", [128, 16, 128], BF16)
    bd = sb("bd", [32, 16, 32], BF16)
    sq = sb("sq", [128, 2, TX], F32)
    acc = sb("acc", [128, TX], F32)
    ones_f = sb("ones_f", [128, 128], F32)
    hist = sb("hist", [128, 16, 4], F32)
    bnst = sb("bnst", [128, 4, 6], F32)
    bnag = sb("bnag", [128, 2], F32)
    rsd = sb("rsd", [128, 1], F32)
    epsb = sb("epsb", [128, 1], F32)
    ps = es.enter_context(nc.psum_tensor("ps", [128, 8, 512], F32))

    def sview(name, ap, nbytes=1 << 20):
        return View(name, 0, nbytes, ap)

    PE, ACT, DVE, POOL, SP = Eng("pe"), Eng("act"), Eng("dve"), Eng("pool"), Eng("sp")
    TR = Tracker()
    slots = []

    def new_slot(name):
        s = Src(f"{name}_{len(slots)}", 16)
        slots.append(s)
        return s

    def do(E, fn, reads=(), writes=(), mark=True, slot=None):
        toks = []
        for v in reads:
            toks += TR.deps(v, "r", E)
        for v in writes:
            toks += TR.deps(v, "w", E)
        E.wait(toks)
        tok = E.emit(fn, mark, slot)
        if tok is not None:
            for v in reads:
                TR.record(v, "r", tok)
            for v in writes:
                TR.record(v, "w", tok)
        return tok

    def psv(b0, b1, a=0, b=512):
        lo = b0 * 2048
        hi = b1 * 2048
        if b1 - b0 == 1:
            return View("ps", lo, hi, ps[:, b0, a:b])
        return View("ps", lo, hi, ps[:, b0:b1, a:b])

    pair_ctr = [0]

    def next_pair():
        p = pair_ctr[0] % 3
        pair_ctr[0] += 1
        return p

    ring_ctr = [0]
    extra = [a_buf3(R1O + 24576 + i * 8192, 16, 256, BF16) for i in range(5)]
    ring_all = [(ring[i], new_slot(f"ring{i}")) for i in range(NRING)] + [(extra[i], new_slot(f"xring{i}")) for i in range(5)]
    active = [ring_all[:NRING]]

    def set_ring(extras):
        active[0] = ring_all[:NRING] + [ring_all[NRING + i] for i in extras]

    def load_wblock(W2d, r0, nk, c0, ncols=256):
        lst = active[0]
        buf, slot = lst[ring_ctr[0] % len(lst)]
        ring_ctr[0] += 1
        src = W2d[r0:r0 + nk * 128, c0:c0 + ncols].rearrange("(k p) f -> p k f", p=128)
        v = buf.v(0, 0, ncols, c2=nk)
        do(POOL, lambda e, v=v, src=src: e.dma_start(out=v.ap, in_=src), writes=[v], slot=slot)
        return v

    identv = sview("ident", ident[:])
    onesv = sview("ones", ones_bf[:])
    prmv = sview("prm", prm_fm[:])
    lnbv = sview("lnb", lnb_t[:])
    rowsv = sview("rows", rows[:])
    rowssv = sview("rows_s", rows_s[:])
    wstv = sview("wst", wst[:])
    bdv = sview("bd", bd[:])
    histv = sview("hist", hist[:])
    bnstv = sview("bnst", bnst[:])
    bnagv = sview("bnag", bnag[:])
    rsdv = sview("rsd", rsd[:])
    epsv = sview("eps", epsb[:])

    def sqv(i, n):
        return View("sq", i * TX * 4, (i * TX + n) * 4, sq[:, i, 0:n])
    accv = sview("acc", acc[:])
    rstd, rstdv = acc, accv
    stcb = a_buf3(R2O + 32768, 16, 6, F32)
    stcf, stcfv = stcb.ap, stcb.v(0, c2=16)
    onesfv = sview("ones_f", ones_f[:])

    scr = a_buf3(R2O, 4, D, F32)

    def setup_early():
        do(POOL, lambda e: e.memset(ident[:], 0.0), writes=[identv])
        do(POOL, lambda e: e.affine_select(out=ident[:], in_=ident[:], compare_op=ALU.not_equal, fill=1.0,
                                           base=0, pattern=[[-1, 128]], channel_multiplier=1),
           reads=[identv], writes=[identv])
        do(POOL, lambda e: e.memset(ones_bf[:], 1.0), writes=[onesv])
        do(POOL, lambda e: e.memset(ones_f[:], 1.0), writes=[onesfv])
        do(POOL, lambda e: e.memset(epsb[:], EPS), writes=[epsv])

    def setup_prm():
        sc0 = stg.v(0)
        do(SP, lambda e: e.dma_start(out=stg.ap[0:16, 0, :], in_=prm), writes=[sc0], slot=new_slot("prm"))

        def fn_prm_tr(e):
            for c in range(16):
                ins = e.transpose(ps[:, 0, c * 16:(c + 1) * 16], stg.ap[0:16, 0, c * 128:(c + 1) * 128], ident[0:16, 0:16])
            return ins
        do(PE, fn_prm_tr, reads=[sc0, identv], writes=[psv(0, 1)])
        do(DVE, lambda e: e.tensor_copy(out=prm_fm[:], in_=ps[:, 0, 0:256].rearrange("p (c r) -> p c r", c=16)[:, :, 0:12]),
           reads=[psv(0, 1)], writes=[prmv])

    def setup_late_a():
        sc0, sc1, sc2 = scr.v(0), scr.v(1), scr.v(2)
        lv0 = View("lnb", 0, D * 4, lnb_t[:, 0, :])
        lv1 = View("lnb", D * 4, 2 * D * 4, lnb_t[:, 1, :])
        do(SP, lambda e: e.dma_start(out=lnb_t[:, 0, :], in_=lnv[0].partition_broadcast(128)), writes=[lv0], slot=new_slot("ln0"))
        do(SP, lambda e: e.dma_start(out=lnb_t[:, 1, :], in_=lnv[1].partition_broadcast(128)), writes=[lv1], slot=new_slot("ln1"))
        do(SP, lambda e: e.dma_start(out=scr.ap[0:2, 1, :], in_=brow), writes=[sc1], slot=new_slot("brow"))
        hi_bf = scr.ap[:, 2, 0:D // 2].bitcast(BF16)
        lo_bf = scr.ap[:, 2, D // 2:D].bitcast(BF16)
        h2 = View("ar", scr.base + 2 * D * 4, scr.base + 2 * D * 4 + D * 2, hi_bf)
        l2 = View("ar", scr.base + 2 * D * 4 + D * 2, scr.base + 3 * D * 4, lo_bf)
        do(DVE, lambda e: e.tensor_copy(out=hi_bf[0:2, :], in_=scr.ap[0:2, 1, :]), reads=[sc1], writes=[h2])
        do(DVE, lambda e: e.tensor_copy(out=scr.ap[0:2, 0, :], in_=hi_bf[0:2, :]), reads=[h2], writes=[sc0])
        do(DVE, lambda e: e.tensor_tensor(out=scr.ap[0:2, 0, :], in0=scr.ap[0:2, 1, :], in1=scr.ap[0:2, 0, :],
                                          op=ALU.subtract), reads=[sc1, sc0], writes=[sc0])
        do(DVE, lambda e: e.tensor_copy(out=lo_bf[0:2, :], in_=scr.ap[0:2, 0, :]), reads=[sc0], writes=[l2])
        for (dst, src, rv) in ((rows[0:1, :], hi_bf[0:1, :], h2), (rows[1:2, :], lo_bf[0:1, :], l2),
                               (rows[32:33, :], hi_bf[1:2, :], h2), (rows[33:34, :], lo_bf[1:2, :], l2)):
            do(SP, lambda e, dst=dst, src=src: e.dma_start(out=dst, in_=src), reads=[rv], writes=[rowsv],
               slot=new_slot("rows"))
        for (r, src, rv) in ((0, hi_bf, h2), (1, lo_bf, l2)):
            do(SP, lambda e, r=r, src=src: e.dma_start(
                out=rows_s[r:r + 1, :].rearrange("p (g i) -> p g i", g=16),
                in_=src[1:2, :].rearrange("p (g i) -> p g i", g=16)[:, :, 0:16]),
               reads=[rv], writes=[rowssv], slot=new_slot("rows_s"))
        wsf = scr.ap[:, 0, :].rearrange("p (g j) -> p g j", g=16)
        do(SP, lambda e: e.dma_start(out=wsf, in_=w_s.rearrange("g i j -> i g j")), writes=[sc0], slot=new_slot("ws"))
        s16 = scr.ap[0:16, 2, 0:512].rearrange("p (g k j) -> p g k j", g=16, k=2)
        for k in range(2):
            do(SP, lambda e, k=k: e.dma_start(out=s16[:, :, k, :], in_=w_s[:, 0:16, 0:16].rearrange("g i j -> i g j")),
               writes=[sc2], slot=new_slot("s16"))

    def setup_late_b():
        sc0, sc1, sc2 = scr.v(0), scr.v(1), scr.v(2)
        wsf = scr.ap[:, 0, :].rearrange("p (g j) -> p g j", g=16)
        wstf = scr.ap[:, 1, :].rearrange("p (g j) -> p g j", g=16)
        for half in range(2):
            def fn_ws_tr(e, half=half):
                for gg in range(8):
                    g = half * 8 + gg
                    ins = e.transpose(ps[:, 6 + gg // 4, (gg % 4) * 128:(gg % 4 + 1) * 128], wsf[:, g, :], ident[:])
                return ins
            do(PE, fn_ws_tr, reads=[sc0, identv], writes=[psv(6, 8)])
            do(ACT, lambda e, half=half: e.activation(out=scr.ap[:, 1, half * 1024:(half + 1) * 1024],
                                                      in_=ps[:, 6:8, :].rearrange("p b n -> p (b n)"), func=AF.Copy),
               reads=[psv(6, 8)], writes=[scr.v(1, half * 1024, (half + 1) * 1024)])
        do(POOL, lambda e: e.affine_select(out=wst[:], in_=wstf, compare_op=ALU.is_ge, fill=0.0, base=0,
                                           pattern=[[0, 16], [1, 128]], channel_multiplier=-1),
           reads=[sc1], writes=[wstv])

        def fn_bd_tr(e):
            for g in range(16):
                ins = e.transpose(ps[0:32, 6, g * 16:(g + 1) * 16],
                                  scr.ap[0:16, 2, g * 32:(g + 1) * 32], ident[0:16, 0:16])
            return ins
        do(PE, fn_bd_tr, reads=[sc2, identv], writes=[psv(6, 7)])
        bdf = scr.ap[0:32, 0, 0:512].rearrange("p (g q) -> p g q", g=16)
        t2 = ps[0:32, 6, 0:256].rearrange("p (g i) -> p g i", g=16)
        do(DVE, lambda e: e.tensor_copy(out=bdf[:, :, 0:16], in_=t2), reads=[psv(6, 7)], writes=[sc0])
        do(DVE, lambda e: e.tensor_copy(out=bdf[:, :, 16:32], in_=t2), reads=[psv(6, 7)], writes=[sc0])
        do(POOL, lambda e: e.affine_select(out=bdf, in_=bdf, compare_op=ALU.is_ge, fill=0.0, base=0,
                                           pattern=[[0, 16], [1, 32]], channel_multiplier=-1),
           reads=[sc0], writes=[sc0])
        do(POOL, lambda e: e.memset(bdf[0:16, :, 16:32], 0.0), reads=[sc0], writes=[sc0])
        do(DVE, lambda e: e.tensor_copy(out=bd[:], in_=bdf), reads=[sc0], writes=[bdv])

    def setup_late():
        setup_late_a()
        setup_late_b()

    def stats_chunk(c, src, n, first, last, on_pe=False, sq_on_dve=False):
        nb = n // 2
        s_ = sqv(c % 2, n)
        if sq_on_dve:
            do(DVE, lambda e: e.tensor_tensor(out=s_.ap, in0=src.ap, in1=src.ap, op=ALU.mult), reads=[src], writes=[s_])
        elif len(src.ap.shape) == 3:
            do(ACT, lambda e: e.activation(out=two(s_.ap), in_=src.ap, func=AF.Square), reads=[src], writes=[s_])
        else:
            do(ACT, lambda e: e.activation(out=s_.ap, in_=src.ap, func=AF.Square), reads=[src], writes=[s_])
        if on_pe:
            def fn(e):
                for tb in range(2):
                    ins = e.matmul(ps[:, 6 + tb, 0:nb], lhsT=ones_f[:], rhs=s_.ap[:, tb * nb:(tb + 1) * nb],
                                   start=first, stop=last)
                return ins
            do(PE, fn, reads=[s_, onesfv], writes=[psv(6, 8, 0, nb)])
        elif first:
            do(DVE, lambda e: e.tensor_copy(out=acc[:, 0:n], in_=s_.ap), reads=[s_], writes=[accv])
        else:
            do(DVE, lambda e: e.tensor_tensor(out=acc[:, 0:n], in0=acc[:, 0:n], in1=s_.ap, op=ALU.add),
               reads=[s_, accv], writes=[accv])

    def stats_finish(n, on_pe=False):
        nb = n // 2
        if not on_pe:
            def fn(e):
                for tb in range(2):
                    ins = e.matmul(ps[:, 6 + tb, 0:nb], lhsT=ones_f[:], rhs=acc[:, tb * nb:(tb + 1) * nb], start=True, stop=True)
                return ins
            do(PE, fn, reads=[accv, onesfv], writes=[psv(6, 8, 0, nb)])
        do(ACT, lambda e: e.activation(out=two(rstd[:, 0:n]), in_=ps[:, 6:8, 0:nb], func=AF.Ln, scale=1.0 / D,
                                       bias=epsb[:, 0:1]), reads=[psv(6, 8, 0, nb), epsv], writes=[rstdv])
        do(ACT, lambda e: e.activation(out=rstd[:, 0:n], in_=rstd[:, 0:n], func=AF.Exp, scale=-0.5),
           reads=[rstdv], writes=[rstdv])

    def pre_norm(row, n):
        for c in range(16):
            stats_chunk(c, xfm.v(c, 0, n), n, c == 0, c == 15, on_pe=True)
        stats_finish(n, on_pe=True)
        for c in range(16):
            hv = hbuf.v(c, 0, n)
            xv = xfm.v(c, 0, n)
            do(DVE, lambda e, hv=hv, xv=xv, c=c: e.scalar_tensor_tensor(
                out=hv.ap, in0=xv.ap, scalar=prm_fm[:, c, row:row + 1], in1=rstd[:, 0:n], op0=ALU.mult, op1=ALU.mult),
               reads=[xv, prmv, rstdv], writes=[hv])

    def post_consume(c, pr):
        pv = psv(2 * pr, 2 * pr + 2, 0, T // 2)
        tv = tmpb.v(c)
        do(DVE, lambda e: e.tensor_copy(out=two(tv.ap), in_=pv.ap), reads=[pv], writes=[tv])
        stats_chunk(c, pv, T, c == 0, c == 15)

    def post_finish(row):
        stats_finish(T)

        def mul(c):
            tv = tmpb.v(c)
            do(DVE, lambda e: e.scalar_tensor_tensor(
                out=tv.ap, in0=tv.ap, scalar=prm_fm[:, c, row:row + 1], in1=rstd[:, 0:T], op0=ALU.mult, op1=ALU.mult),
               reads=[tv, prmv, rstdv], writes=[tv])

        def add(c):
            tv = tmpb.v(c)
            xv = xfm.v(c, 0, T)
            do(DVE, lambda e: e.tensor_tensor(out=xv.ap, in0=xv.ap, in1=tv.ap, op=ALU.add),
               reads=[tv, xv], writes=[xv])
        mul(0)
        for c in range(16):
            if c + 1 < 16:
                mul(c + 1)
            add(c)

    def linear_ws(W2d, K, c0_list, rhs, n, consume, progressive=False):
        nb = n // 2
        KC = K // 128
        subs = [(s, min(16, KC - s)) for s in range(0, KC, 16)]
        for bi, c0 in enumerate(c0_list):
            pairs = [next_pair(), next_pair()]
            for (s, nk) in subs:
                wv_ = load_wblock(W2d, s * 128, nk, c0)
                if bi == 0 and progressive and KC == 16:
                    for k in range(nk):
                        def fnk(e, wv_=wv_, k=k, pairs=tuple(pairs)):
                            for fc in range(2):
                                for tb in range(2):
                                    ins = e.matmul(ps[:, 2 * pairs[fc] + tb, 0:nb], lhsT=wv_.ap[:, k, fc * 128:(fc + 1) * 128],
                                                   rhs=rhs(k).ap[:, tb * nb:(tb + 1) * nb], start=(k == 0), stop=(k == KC - 1))
                            return ins
                        do(PE, fnk, reads=[wv_, rhs(k)],
                           writes=[psv(2 * pairs[0], 2 * pairs[0] + 2, 0, nb), psv(2 * pairs[1], 2 * pairs[1] + 2, 0, nb)])
                    continue
                for fc in range(2):
                    pr = pairs[fc]

                    def fn(e, wv_=wv_, fc=fc, pr=pr, s=s, nk=nk):
                        for tb in range(2):
                            for k in range(nk):
                                ins = e.matmul(ps[:, 2 * pr + tb, 0:nb], lhsT=wv_.ap[:, k, fc * 128:(fc + 1) * 128],
                                               rhs=rhs(s + k).ap[:, tb * nb:(tb + 1) * nb],
                                               start=(s + k == 0), stop=(s + k == KC - 1))
                        return ins
                    do(PE, fn, reads=[wv_] + [rhs(s + k) for k in range(nk)], writes=[psv(2 * pr, 2 * pr + 2, 0, nb)])
            for fc in range(2):
                consume(bi, fc, pairs[fc])

    stg_slots = [new_slot(f"stg{i}") for i in range(5)]
    out_slots = [new_slot(f"out{i}") for i in range(5)]
    st_slot = new_slot("stc")
    sg_slot = new_slot("sgu")
    wv_slots = [new_slot(f"wv{i}") for i in range(4)]

    lst = a_buf3(R2O, 4, D, F32)
    l4 = a_buf3(HO, 1, D, F32)

    def load_dma(p):
        for t in range(4):
            v = lst.v(t)
            do(SP, lambda e, t=t, v=v: e.dma_start(out=v.ap, in_=xin[p, t * 128:(t + 1) * 128, :]), writes=[v],
               slot=stg_slots[t])
        do(SP, lambda e: e.dma_start(out=l4.ap[0:40, 0, :], in_=xin[p, 512:552, :]), writes=[l4.v(0)], slot=stg_slots[4])

    def load_compute(p):
        v4 = l4.v(0)
        for c in range(16):
            bank = c % 6

            def fn(e, c=c, bank=bank):
                for t in range(4):
                    ins = e.transpose(ps[:, bank, t * 128:(t + 1) * 128], lst.ap[:, t, c * 128:(c + 1) * 128], ident[:])
                return ins
            do(PE, fn, reads=[lst.v(0, c2=4), identv], writes=[psv(bank, bank + 1)])
            xv = xfm.v(c, 0, 512)
            E = ACT if c % 2 == 0 else DVE
            if E is ACT:
                do(ACT, lambda e, xv=xv, bank=bank: e.activation(out=xv.ap, in_=ps[:, bank, :], func=AF.Copy),
                   reads=[psv(bank, bank + 1)], writes=[xv])
            else:
                do(DVE, lambda e, xv=xv, bank=bank: e.tensor_copy(out=xv.ap, in_=ps[:, bank, :]),
                   reads=[psv(bank, bank + 1)], writes=[xv])

        def fn4(e):
            for c in range(16):
                ins = e.transpose(ps[:, 6 + c // 8, (c % 8) * 40:(c % 8 + 1) * 40], l4.ap[0:40, 0, c * 128:(c + 1) * 128],
                                  ident[0:40, 0:40])
            return ins
        do(PE, fn4, reads=[v4, identv], writes=[psv(6, 8)])
        for hb in range(2):
            src = ps[:, 6 + hb, 0:320].rearrange("p (c r) -> p c r", c=8)
            c0 = hb * 8
            xs = xfm.v(c0, 512, 546, c2=c0 + 8)
            do(DVE, lambda e, src=src, c0=c0: e.tensor_copy(out=xfm.ap[:, c0:c0 + 8, 512:546], in_=src[:, :, 0:34]),
               reads=[psv(6 + hb, 7 + hb)], writes=[xs])
            do(DVE, lambda e, src=src, c0=c0: e.tensor_copy(out=hist[:, c0:c0 + 8, :], in_=src[:, :, 34:38]),
               reads=[psv(6 + hb, 7 + hb)], writes=[histv])

    def stage_store(p):
        for t in range(5):
            M = 128 if t < 4 else 32
            sv = stg.v(t)
            for cg in range(4):
                bank = (t * 4 + cg) % 6

                def fn(e, t=t, cg=cg, bank=bank, M=M):
                    for cc in range(4):
                        c = cg * 4 + cc
                        ins = e.transpose(ps[0:M, bank, cc * 128:(cc + 1) * 128], xfm.ap[:, c, t * 128:t * 128 + M], ident[:])
                    return ins
                do(PE, fn, reads=[xfm.v(cg * 4, t * 128, t * 128 + M, c2=cg * 4 + 4), identv], writes=[psv(bank, bank + 1)])
                E = ACT if cg % 2 == 0 else DVE
                if E is ACT:
                    do(ACT, lambda e, t=t, cg=cg, bank=bank, M=M: e.activation(
                        out=stg.ap[0:M, t, cg * 512:(cg + 1) * 512], in_=ps[0:M, bank, :], func=AF.Copy),
                       reads=[psv(bank, bank + 1)], writes=[sv])
                else:
                    do(DVE, lambda e, t=t, cg=cg, bank=bank, M=M: e.tensor_copy(
                        out=stg.ap[0:M, t, cg * 512:(cg + 1) * 512], in_=ps[0:M, bank, :]),
                       reads=[psv(bank, bank + 1)], writes=[sv])
            do(SP, lambda e, t=t, M=M: e.dma_start(out=yout[p, t * 128:t * 128 + M, :], in_=stg.ap[0:M, t, :]),
               reads=[sv], slot=out_slots[t])

    def mixer_out(W2d, layer):
        def consume(bi, fc, pr):
            post_consume(2 * bi + fc, pr)
        linear_ws(W2d, D, [256 * q for q in range(8)], lambda k: ybuf.v(k), T, consume)
        post_finish(R_MIXPOST + 4 * layer)

    def stage_conv(p, hook=None):
        CL = int(os.environ.get("MK_CONV_LVL") or 9)
        set_ring([2, 3, 4])
        for c in range(16):
            xv = xfm.v(c, 0, TX)
            hv = hbuf.v(c, 0, TX)
            do(ACT, lambda e, xv=xv, hv=hv, c=c: e.activation(out=hv.ap, in_=xv.ap, func=AF.Copy,
                                                              scale=prm_fm[:, c, R_MIXPRE:R_MIXPRE + 1]),
               reads=[xv, prmv], writes=[hv])
            stats_chunk(c, xv, TX, c == 0, c == 15, on_pe=True, sq_on_dve=True)
        stats_finish(TX, on_pe=True)
        r2v = ctmp7.v(6, 0, TX)
        r2 = ctmp7.ap[:, 6, 0:TX]
        do(DVE, lambda e: e.tensor_tensor(out=r2, in0=rstd[:, 0:TX], in1=rstd[:, 0:TX], op=ALU.mult),
           reads=[rstdv], writes=[r2v])
        if CL <= 1:
            return
        nbx = TX // 2
        GC, CZ, CV = 0, 2, 4

        def consume(bi, fc, pr):
            kind, q = bi % 3, bi // 3
            c = 2 * q + fc
            pv = psv(2 * pr, 2 * pr + 2, 0, nbx)
            if CL <= 2:
                gv = ctmp.v(GC + fc, 0, TX)
                do(ACT, lambda e: e.activation(out=two(gv.ap), in_=pv.ap, func=AF.Copy), reads=[pv], writes=[gv])
                return
            if kind == 0:
                gv = ctmp.v(GC + fc, 0, TX)
                do(ACT, lambda e: e.activation(out=two(gv.ap), in_=pv.ap, func=AF.Copy), reads=[pv], writes=[gv])
            elif kind == 1:
                gv = ctmp.v(GC + fc, 0, TX)
                cz = ctmp.v(CZ + fc, 0, 550)
                gsb = ctmp.ap[:, GC + fc, :]
                czb = ctmp.ap[:, CZ + fc, :]
                zrv = ctmp.v(CV + fc, 0, TX)
                zr = ctmp.ap[:, CV + fc, :]
                do(DVE, lambda e: e.tensor_tensor(out=two(zrv.ap), in0=pv.ap, in1=two(r2), op=ALU.mult),
                   reads=[pv, r2v], writes=[zrv])
                do(DVE, lambda e: e.tensor_tensor(out=czb[:, 2:514], in0=zr[:, 0:512], in1=gsb[:, 0:512],
                                                  op=ALU.mult), reads=[zrv, gv], writes=[cz])
                do(DVE, lambda e: e.tensor_tensor(
                    out=czb[:, 514:550].rearrange("p (s r) -> p s r", s=2)[:, :, 2:18],
                    in0=zr[:, 512:544].rearrange("p (s r) -> p s r", s=2),
                    in1=gsb[:, 512:544].rearrange("p (s r) -> p s r", s=2), op=ALU.mult),
                   reads=[zrv, gv], writes=[cz])
                do(DVE, lambda e: e.tensor_tensor(out=czb[:, 0:2], in0=zr[:, 544:546], in1=gsb[:, 544:546],
                                                  op=ALU.mult), reads=[zrv, gv], writes=[cz])
                do(DVE, lambda e: e.tensor_copy(
                    out=czb[:, 514:550].rearrange("p (s r) -> p s r", s=2)[:, :, 0:2],
                    in_=hist[:, c, :].rearrange("p (s r) -> p s r", s=2)), reads=[histv], writes=[cz])
                do(DVE, lambda e: e.tensor_copy(
                    out=stcf[:, c, :].rearrange("p (s r) -> p s r", s=3),
                    in_=czb[:, 496:550].rearrange("p (s r) -> p s r", s=3)[:, :, 16:18]), reads=[cz], writes=[stcfv])
                cv = ctmp.v(CV + fc, 0, 548)
                cvb = ctmp.ap[:, CV + fc, :]
                w0 = prm_fm[:, c, R_CONVW:R_CONVW + 1]
                w1 = prm_fm[:, c, R_CONVW + 1:R_CONVW + 2]
                w2 = prm_fm[:, c, R_CONVW + 2:R_CONVW + 3]
                do(DVE, lambda e: e.tensor_scalar(out=cvb[:, 0:548], in0=czb[:, 0:548], scalar1=w0, scalar2=None,
                                                  op0=ALU.mult), reads=[cz, prmv], writes=[cv])
                do(DVE, lambda e: e.scalar_tensor_tensor(out=cvb[:, 0:548], in0=czb[:, 1:549], scalar=w1, in1=cvb[:, 0:548],
                                                         op0=ALU.mult, op1=ALU.add), reads=[cz, cv, prmv], writes=[cv])
                do(DVE, lambda e: e.scalar_tensor_tensor(out=cvb[:, 0:548], in0=czb[:, 2:550], scalar=w2, in1=cvb[:, 0:548],
                                                         op0=ALU.mult, op1=ALU.add), reads=[cz, cv, prmv], writes=[cv])
            else:
                cv = ctmp.v(CV + fc, 0, 548)
                cvb = ctmp.ap[:, CV + fc, :]
                yv = ybuf.v(c)
                yb = ybuf.ap[:, c, :]
                gbv = ctmp.v(GC + fc, 0, TX)
                gb = ctmp.ap[:, GC + fc, :]
                do(DVE, lambda e: e.tensor_tensor(out=two(gbv.ap), in0=pv.ap, in1=two(rstd[:, 0:TX]), op=ALU.mult),
                   reads=[pv, rstdv], writes=[gbv])
                do(DVE, lambda e: e.tensor_tensor(out=yb[:, 0:512], in0=gb[:, 0:512], in1=cvb[:, 0:512], op=ALU.mult),
                   reads=[gbv, cv], writes=[yv])
                do(DVE, lambda e: e.tensor_tensor(
                    out=yb[:, 512:544].rearrange("p (s r) -> p s r", s=2),
                    in0=gb[:, 512:544].rearrange("p (s r) -> p s r", s=2),
                    in1=cvb[:, 514:550].rearrange("p (s r) -> p s r", s=2)[:, :, 0:16], op=ALU.mult),
                   reads=[gbv, cv], writes=[yv])

        c0s = []
        for q in range(8):
            c0s += [D + 256 * q, 2 * D + 256 * q, 256 * q]
        linear_ws(a_w_in, D, c0s, lambda k: hbuf.v(k, 0, TX), TX, consume, progressive=True)
        if hook is not None:
            hook()
        if CL <= 3:
            return

        for half in range(2):
            def fn_h(e, half=half):
                for cc in range(8):
                    c = half * 8 + cc
                    ins = e.transpose(ps[0:6, 6 + cc // 4, (cc % 4) * 128:(cc % 4 + 1) * 128], stcf[:, c, :], ident[:])
                return ins
            do(PE, fn_h, reads=[stcfv, identv], writes=[psv(6, 8)])
            sv = stg.v(4, half * 1024, half * 1024 + 1024)
            do(ACT, lambda e, half=half: e.activation(
                out=stg.ap[0:6, 4, half * 1024:(half + 1) * 1024],
                in_=ps[0:6, 6:8, :].rearrange("p b n -> p (b n)"), func=AF.Copy), reads=[psv(6, 8)], writes=[sv])
        do(SP, lambda e: e.dma_start(out=stc_out[p], in_=stg.ap[0:6, 4, :]), reads=[stg.v(4)], slot=st_slot)
        if CL <= 4:
            return
        if CL <= 6:
            def consume_dbg(bi, fc, pr):
                c = 2 * bi + fc
                pv = psv(2 * pr, 2 * pr + 2, 0, T // 2)
                tv = tmpb.v(c)
                do(DVE, lambda e: e.tensor_copy(out=two(tv.ap), in_=pv.ap), reads=[pv], writes=[tv])
                if CL == 6:
                    stats_chunk(c, pv, T, c == 0, c == 15)
            linear_ws(a_w_out, D, [256 * q for q in range(8)], lambda k: ybuf.v(k), T, consume_dbg)
            if CL == 6:
                stats_finish(T)
            return
        mixer_out(a_w_out, 0)

    def stage_ffn(p, layer, hook=None):
        set_ring([3, 4])
        row = R_FFNPRE + 4 * layer
        for c in range(16):
            xv = xfm.v(c, 0, T)
            hv = hbuf.v(c, 0, T)
            do(ACT, lambda e, xv=xv, hv=hv, c=c: e.activation(out=hv.ap, in_=xv.ap, func=AF.Copy,
                                                              scale=prm_fm[:, c, row:row + 1]),
               reads=[xv, prmv], writes=[hv])
            stats_chunk(c, xv, T, c == 0, c == 15, on_pe=True)
        stats_finish(T, on_pe=True)
        hb = T // 2
        sgb4 = a_buf3(R2O + 24576, 4, T, F32)

        def consume(bi, fc, pr):
            kind, q = bi % 2, bi // 2
            f = 2 * q + fc
            pv = psv(2 * pr, 2 * pr + 2, 0, hb)
            sv = sgb4.v(fc)
            if kind == 0:
                do(DVE, lambda e: e.tensor_tensor(out=two(sv.ap), in0=pv.ap, in1=two(rstd[:, 0:T]), op=ALU.mult),
                   reads=[pv, rstdv], writes=[sv])
                do(ACT, lambda e: e.activation(out=sv.ap, in_=sv.ap, func=AF.Silu), reads=[sv], writes=[sv])
            else:
                gv = gbuf.v(f)
                uv = sgb4.v(2 + fc)
                do(DVE, lambda e: e.tensor_tensor(out=two(uv.ap), in0=pv.ap, in1=two(rstd[:, 0:T]), op=ALU.mult),
                   reads=[pv, rstdv], writes=[uv])
                do(DVE, lambda e: e.tensor_tensor(out=gv.ap, in0=uv.ap, in1=sv.ap, op=ALU.mult),
                   reads=[uv, sv], writes=[gv])

        c0s = []
        for q in range(22):
            c0s += [("g", 256 * q), ("u", 256 * q)]
        nb = hb
        for bi, (which, c0) in enumerate(c0s):
            W2d = w_gate[layer] if which == "g" else w_up[layer]
            linear_ws(W2d, D, [c0], lambda k: hbuf.v(k, 0, T), T, lambda _b, fc, pr, bi=bi: consume(bi, fc, pr),
                      progressive=(bi == 0))
            if hook is not None and bi == 1:
                setup_late_a()
            if hook is not None and bi == 33:
                setup_late_b()

        def consume_d(bi, fc, pr):
            post_consume(2 * bi + fc, pr)
        linear_ws(w_down[layer], DFF, [256 * q for q in range(8)], lambda k: gbuf.v(k), T, consume_d)
        post_finish(R_FFNPOST + 4 * layer)

    def stage_sgu(p):
        set_ring([])
        pre_norm(R_MIXPRE + 4, T)
        for n in range(4):
            v = wvb.v(0, n * 512, (n + 1) * 512, c2=16)
            src = b_w_in[:, D + n * 512:D + (n + 1) * 512].rearrange("(k p) f -> p k f", p=128)
            do(POOL, lambda e, v=v, src=src: e.dma_start(out=v.ap, in_=src), writes=[v], slot=wv_slots[n])
        def v_group(m, n):
            M = 128 if m < 4 else 32
            t0 = m * 128
            qb = 4 * (m % 2)

            def fn(e):
                for k in range(16):
                    e.matmul(ps[0:M, qb + n, :], lhsT=hbuf.ap[:, k, t0:t0 + M], rhs=wvb.ap[:, k, n * 512:(n + 1) * 512],
                             start=(k == 0), stop=False)
                ins = e.matmul(ps[0:M, qb + n, :], lhsT=ones_bf[0:2, 0:M], rhs=rows[0:2, n * 512:(n + 1) * 512],
                               start=False, stop=True)
                return ins
            do(PE, fn, reads=[hbuf.v(0, t0, t0 + M, c2=16), wvb.v(0, n * 512, (n + 1) * 512, c2=16), onesv, rowsv],
               writes=[psv(qb + n, qb + n + 1)])

        def v_post(m):
            M = 128 if m < 4 else 32
            qb = 4 * (m % 2)
            pq = psv(qb, qb + 4)
            vr = vraw.v(0)
            vra = vraw.ap[0:M, 0, :]
            do(ACT, lambda e: e.activation(out=vra, in_=ps[0:M, qb:qb + 4, :].rearrange("p b n -> p (b n)"), func=AF.Gelu),
               reads=[pq], writes=[vr])
            for n in range(4):
                do(DVE, lambda e, n=n: e.bn_stats(out=bnst[0:M, n, :], in_=vra[:, n * 512:(n + 1) * 512]),
                   reads=[vr], writes=[bnstv])
            do(DVE, lambda e: e.bn_aggr(out=bnag[0:M, :], in_=bnst[0:M, :, :]), reads=[bnstv], writes=[bnagv])
            do(ACT, lambda e: e.activation(out=rsd[0:M, :], in_=bnag[0:M, 1:2], func=AF.Sqrt, scale=1.0,
                                           bias=epsb[0:M, 0:1]), reads=[bnagv, epsv], writes=[rsdv])
            do(DVE, lambda e: e.reciprocal(out=rsd[0:M, :], in_=rsd[0:M, :]), reads=[rsdv], writes=[rsdv])
            do(DVE, lambda e: e.scalar_tensor_tensor(out=vra, in0=vra, scalar=bnag[0:M, 0:1], in1=lnb_t[0:M, 0, :],
                                                     op0=ALU.subtract, op1=ALU.mult), reads=[vr, bnagv, lnbv], writes=[vr])
            vb = vbf.v(m)
            if m < 4:
                do(DVE, lambda e: e.scalar_tensor_tensor(out=vbf.ap[0:M, m, :], in0=vra, scalar=rsd[0:M, :],
                                                         in1=lnb_t[0:M, 1, :], op0=ALU.mult, op1=ALU.add),
                   reads=[vr, rsdv, lnbv], writes=[vb])
            else:
                do(DVE, lambda e: e.scalar_tensor_tensor(out=vra, in0=vra, scalar=rsd[0:M, :], in1=lnb_t[0:M, 1, :],
                                                         op0=ALU.mult, op1=ALU.add), reads=[vr, rsdv, lnbv], writes=[vr])
                do(ACT, lambda e: e.activation(out=vbf.ap[0:M, m, :], in_=vra, func=AF.Copy), reads=[vr], writes=[vb])
                do(SP, lambda e: e.dma_start(out=sgu_out[p], in_=vra), reads=[vr], slot=sg_slot)

        for k in range(16):
            def fnk(e, k=k):
                for m in (0, 1):
                    ins = e.matmul(ps[:, 4 * m, :], lhsT=hbuf.ap[:, k, m * 128:(m + 1) * 128], rhs=wvb.ap[:, k, 0:512],
                                   start=(k == 0), stop=False)
                return ins
            do(PE, fnk, reads=[hbuf.v(k, 0, 256), wvb.v(k, 0, 512)], writes=[psv(0, 1), psv(4, 5)])

        def fnb(e):
            for m in (0, 1):
                ins = e.matmul(ps[:, 4 * m, :], lhsT=ones_bf[0:2, 0:128], rhs=rows[0:2, 0:512], start=False, stop=True)
            return ins
        do(PE, fnb, reads=[onesv, rowsv], writes=[psv(0, 1), psv(4, 5)])
        for n in range(1, 4):
            v_group(0, n)
            v_group(1, n)
        v_post(0)
        v_post(1)
        for m in (2, 3, 4):
            for n in range(4):
                v_group(m, n)
            v_post(m)
        hb = T // 2

        def consume_u(bi, fc, pr):
            c = 2 * bi + fc
            pv = psv(2 * pr, 2 * pr + 2, 0, hb)
            uv = ybuf.v(c)
            do(ACT, lambda e: e.activation(out=two(uv.ap), in_=pv.ap, func=AF.Gelu, bias=prm_fm[:, c, R_BU:R_BU + 1]),
               reads=[pv, prmv], writes=[uv])
        linear_ws(b_w_in, D, [256 * q for q in range(8)], lambda k: hbuf.v(k, 0, T), T, consume_u)
        for g in range(16):
            pr = next_pair()
            pm = psv(2 * pr, 2 * pr + 2)

            def fn(e, g=g, pr=pr):
                for m in range(4):
                    o = ps[:, 2 * pr, m * 128:(m + 1) * 128]
                    e.matmul(o, lhsT=vbf.ap[:, m, g * 128:(g + 1) * 128], rhs=wst[:, g, :], start=True, stop=False)
                    e.matmul(o, lhsT=ones_bf[32:34, :], rhs=rows[32:34, g * 128:(g + 1) * 128], start=False, stop=True)
                o = ps[:, 2 * pr + 1, 0:32]
                e.matmul(o, lhsT=vbf.ap[0:32, 4, g * 128:(g + 1) * 128], rhs=bd[:, g, :], start=True, stop=False)
                for s in range(2):
                    os_ = ps[:, 2 * pr + 1, s * 16:(s + 1) * 16]
                    ins = e.matmul(os_, lhsT=ones_bf[0:2, :], rhs=rows_s[0:2, g * 16:(g + 1) * 16], start=False,
                                   stop=(s == 1))
                return ins
            do(PE, fn, reads=[vbf.v(0, c2=5), wstv, bdv, onesv, rowsv, rowssv], writes=[pm])
            yv = ybuf.v(g)
            yb = ybuf.ap[:, g, :]
            do(DVE, lambda e, yb=yb, pr=pr: e.tensor_tensor(out=yb[:, 0:512], in0=ps[:, 2 * pr, :], in1=yb[:, 0:512],
                                                            op=ALU.mult), reads=[pm, yv], writes=[yv])
            do(DVE, lambda e, yb=yb, pr=pr: e.tensor_tensor(out=yb[:, 512:544], in0=ps[:, 2 * pr + 1, 0:32],
                                                            in1=yb[:, 512:544], op=ALU.mult), reads=[pm, yv], writes=[yv])
        set_ring([0, 1, 2, 3, 4])
        mixer_out(b_w_out, 1)

    stages = ["load", "conv", "ffn0", "sgu", "ffn1"]
    setup_early()
    setup_prm()
    load_dma(0)
    for p in range(NPASS):
        load_compute(p)
        if p == 0 and stop_after == "load":
            setup_late()
        if stop_after != "load":
            stage_conv(p)
            if stop_after != "conv":
                stage_ffn(p, 0, True if p == 0 else None)
                if stop_after != "ffn0":
                    stage_sgu(p)
                    if stop_after != "sgu":
                        stage_ffn(p, 1)
        if p + 1 < NPASS:
            load_dma(p + 1)
        stage_store(p)

    final = [(s, s.count) for s in out_slots + [st_slot, sg_slot] if s.count > 0]
    SP.wait(final)

    for s in [PE, ACT, DVE, POOL, SP] + slots:
        s.sem = es.enter_context(nc.semaphore("s_" + s.name))
    block = es.enter_context(nc.Block())

    def replay(E):
        def run(e):
            for op in E.ops:
                if op[0] == "w":
                    e.wait_ge(op[1].sem, op[2] * op[1].inc)
                else:
                    ins = op[1](e)
                    if op[2] is not None:
                        ins.then_inc(op[2].sem, op[2].inc)
        return run

    block.tensor(replay(PE))
    block.scalar(replay(ACT))
    block.vector(replay(DVE))
    block.gpsimd(replay(POOL))
    block.sync(replay(SP))
    es.close()
    return nc, declared


def make_in_maps(inp):
    f = lambda a: np.ascontiguousarray(np.asarray(a, dtype=np.float32))
    x_prompt, x_sample, cache_conv = f(inp["x_prompt"]), f(inp["x_sample"]), f(inp["cache_conv"])
    prm = np.zeros((16, D), np.float32)
    for l in range(2):
        prm[4 * l + 0] = inp["norm_mix_pre"][l]
        prm[4 * l + 1] = inp["norm_mix_post"][l]
        prm[4 * l + 2] = inp["norm_ffn_pre"][l]
        prm[4 * l + 3] = inp["norm_ffn_post"][l]
    prm[8:11] = inp["a_conv_w"][0]
    prm[11] = inp["b_b_in"][0][:D]
    shared = {
        "prm": prm,
        "lnv": f(np.stack([inp["b_ln_g"][0], inp["b_ln_b"][0]])),
        "brow": f(np.stack([inp["b_b_in"][0][D:], inp["b_b_s"][0].reshape(D)])),
        "w_s": f(inp["b_w_s"][0]),
        "a_w_in": f(inp["a_w_in"][0]),
        "a_w_out": f(inp["a_w_out"][0]),
        "b_w_in": f(inp["b_w_in"][0]),
        "b_w_out": f(inp["b_w_out"][0]),
        "w_gate": f(inp["ffn_w_gate"]),
        "w_up": f(inp["ffn_w_up"]),
        "w_down": f(inp["ffn_w_down"]),
    }
    maps = []
    for c in range(NCORES):
        xin = np.zeros((NPASS, XROWS, D), np.float32)
        for p in range(NPASS):
            vc = 2 * c + p
            b, s0 = vc // 4, (vc % 4) * 512
            xin[p, 0:512] = x_prompt[b, s0:s0 + 512]
            xin[p, 512:544] = x_sample[2 * vc:2 * vc + 2].reshape(32, D)
            if s0 > 0:
                xin[p, 544:546] = x_prompt[b, s0 - 2:s0]
            xin[p, 546:550] = cache_conv[0, 2 * vc:2 * vc + 2].reshape(4, D)
        m = dict(shared)
        m["xin"] = xin
        maps.append(m)
    return maps


def assemble(results):
    y_prompt = np.zeros((4, 2048, D), np.float32)
    y_sample = np.zeros((32, 16, D), np.float32)
    st_p = np.zeros((1, 4, 2, D), np.float32)
    st_s = np.zeros((1, 32, 2, D), np.float32)
    sgu = np.zeros((1, 32, 16, D), np.float32)
    for c in range(NCORES):
        r = results[c]
        for p in range(NPASS):
            vc = 2 * c + p
            b, s0 = vc // 4, (vc % 4) * 512
            y_prompt[b, s0:s0 + 512] = r["yout"][p, 0:512]
            y_sample[2 * vc:2 * vc + 2] = r["yout"][p, 512:544].reshape(2, 16, D)
            if vc % 4 == 3:
                st_p[0, b] = r["stc"][p, 0:2]
            st_s[0, 2 * vc:2 * vc + 2] = r["stc"][p, 2:6].reshape(2, 2, D)
            sgu[0, 2 * vc:2 * vc + 2] = r["sgu"][p].reshape(2, 16, D)
    return (y_prompt, y_sample, st_p, st_s, sgu)


def kernel(**inputs):
    nc, declared = build_program(os.environ.get("MK_STOP_AFTER") or None)
    maps = make_in_maps(inputs)
    maps = [{k: m[k] for k in declared} for m in maps]
    ncores = int(os.environ.get("MK_NCORES") or NCORES)
    if os.environ.get("MK_TRACE"):
        res = run_bass_kernel_spmd(nc, maps[:ncores], core_ids=list(range(ncores)), trace=True)
        print("MK_TRACE exec_time_ns", res.exec_time_ns)
    else:
        res = run_bass_kernel_spmd(nc, maps[:ncores], core_ids=list(range(ncores)))
    results = list(res.results) + [res.results[0]] * (NCORES - ncores)
    return assemble(results)
```
